# Optimizing a Trainium2 kernel written in Bass

```python
import functools
import jax, jax.numpy as jnp
from jax import lax
import numpy as np

D_MODEL = 1024
BATCH = 8
SEQ = 2048
DEPTH = 2

GRID_W = 64
CTX_LEN = 256
N_FGROUPS = 4
FGROUP_DIM = 128
F_WIDTH = N_FGROUPS * FGROUP_DIM
HEAD_DIM = 64
N_RHEADS = 8
R_WIDTH = N_RHEADS * HEAD_DIM
D_DECAY_LORA = 64
D_AAA_LORA = 64
D_GATE_LORA = 128
N_DIRS = 2
D_FF = 4 * D_MODEL
RWKV_IN = 3 * R_WIDTH + N_DIRS * (D_DECAY_LORA + D_AAA_LORA) + D_GATE_LORA
IN_WIDTH = F_WIDTH + RWKV_IN + 2 * D_MODEL
N_MOD = 6
NORM_EPS = 1e-6
GN_EPS = 64e-5
L2_EPS = 1e-12
RWKV_SPLITS = [R_WIDTH, 2 * R_WIDTH, 3 * R_WIDTH,
               3 * R_WIDTH + D_DECAY_LORA,
               3 * R_WIDTH + 2 * D_DECAY_LORA,
               3 * R_WIDTH + 2 * D_DECAY_LORA + D_AAA_LORA,
               3 * R_WIDTH + 2 * D_DECAY_LORA + 2 * D_AAA_LORA]

kernel_name = "hybrid_fourier_rwkv7_dit_prefix"


def rms_norm(x, g):
    x32 = x.astype(jnp.float32)
    y = x32 * lax.rsqrt(jnp.mean(x32 * x32, axis=-1, keepdims=True) + NORM_EPS)
    return (y * g.astype(jnp.float32)).astype(x.dtype)


def grid_shift(u):
    b, l, ch = u.shape
    rows = l // GRID_W
    q = u.reshape(b, rows, GRID_W, 4, ch // 4)
    left = jnp.pad(q[:, :, :-1, 0], ((0, 0), (0, 0), (1, 0), (0, 0)))
    right = jnp.pad(q[:, :, 1:, 1], ((0, 0), (0, 0), (0, 1), (0, 0)))
    up = jnp.pad(q[:, :-1, :, 2], ((0, 0), (1, 0), (0, 0), (0, 0)))
    down = jnp.pad(q[:, 1:, :, 3], ((0, 0), (0, 1), (0, 0), (0, 0)))
    return jnp.stack([left, right, up, down], axis=3).reshape(b, l, ch)


def seq_shift(u):
    b, l, ch = u.shape
    q = u.reshape(b, l, 2, ch // 2)
    prev = jnp.pad(q[:, :-1, 0], ((0, 0), (1, 0), (0, 0)))
    nxt = jnp.pad(q[:, 1:, 1], ((0, 0), (0, 1), (0, 0)))
    return jnp.stack([prev, nxt], axis=2).reshape(b, l, ch)


def fourier_mix(u):
    b, l, _ = u.shape
    ug = u.astype(jnp.float32).reshape(b, l, N_FGROUPS, FGROUP_DIM)
    f = jnp.fft.fft2(ug, axes=(1, 3), norm="ortho").real
    return f.reshape(b, l, F_WIDTH).astype(u.dtype)


def rwkv_scan(r, k, v, w, kk, a, s0):
    def step(s, inp):
        r_t, k_t, v_t, w_t, kk_t, a_t = inp
        sa = jnp.einsum('dbhvk,dbhk->dbhv', s, kk_t)
        s = (s * w_t[..., None, :] - sa[..., :, None] * (kk_t * a_t)[..., None, :]
             + v_t[..., :, None] * k_t[..., None, :])
        y = jnp.einsum('dbhvk,dbhk->dbhv', s, r_t)
        return s, y
    xs = tuple(jnp.moveaxis(t.astype(jnp.float32), 2, 0) for t in (r, k, v, w, kk, a))
    s_final, ys = lax.scan(step, s0, xs)
    return jnp.moveaxis(ys, 0, 2), s_final


def _dir_seq(t):
    return jnp.stack([t[0], jnp.flip(t[1], axis=1)], axis=0)


def _both(t):
    return jnp.stack([t, jnp.flip(t, axis=1)], axis=0)


def token_mixer(h, s0, shift_fn, need_out, w_in, mu_shift, w0, w_up, a0, a_up, g_up,
                k_k, k_a, r_k, ln_x_w, ln_x_b, w_fourier_up, w_rwkv_up, w_out):
    b, l, _ = h.shape
    proj = h @ w_in
    f_in = proj[..., :F_WIDTH]
    rw = proj[..., F_WIDTH:F_WIDTH + RWKV_IN]
    gates = proj[..., F_WIDTH + RWKV_IN:]
    rw = rw + mu_shift * (shift_fn(rw) - rw)
    r, k, v, wd_f, wd_b, ad_f, ad_b, gd = jnp.split(rw, RWKV_SPLITS, axis=-1)
    wd = jnp.stack([wd_f, wd_b], axis=0)
    ad = jnp.stack([ad_f, ad_b], axis=0)
    w_logit = w0[:, None, None, :] + jnp.einsum('dblr,drc->dblc', jnp.tanh(wd), w_up)
    decay = jnp.exp(-jnp.exp((-jax.nn.softplus(-w_logit) - 0.5).astype(jnp.float32)))
    a = jax.nn.sigmoid(a0[:, None, None, :] + jnp.einsum('dblr,drc->dblc', ad, a_up))
    k_dir = k[None] * (1 + (a - 1) * k_a)
    g = jax.nn.sigmoid(gd) @ g_up
    kk = (k * k_k).astype(jnp.float32).reshape(b, l, N_RHEADS, HEAD_DIM)
    kk = kk / jnp.maximum(jnp.sqrt(jnp.sum(kk * kk, axis=-1, keepdims=True)), L2_EPS)

    hd = lambda t: t.reshape(t.shape[:-1] + (N_RHEADS, HEAD_DIM))
    ys, s_final = rwkv_scan(_both(hd(r)), _dir_seq(hd(k_dir)), _both(hd(v)),
                            _dir_seq(hd(decay)), _both(kk), _dir_seq(hd(a)), s0)
    if not need_out:
        return None, s_final
    y = ys[0] + jnp.flip(ys[1], axis=1)
    mu = jnp.mean(y, axis=-1, keepdims=True)
    var = jnp.mean(jnp.square(y - mu), axis=-1, keepdims=True)
    yn = ((y - mu) * lax.rsqrt(var + GN_EPS)).reshape(b, l, R_WIDTH)
    yn = (yn * ln_x_w + ln_x_b).astype(h.dtype)
    bonus = jnp.sum(hd(r) * hd(k_dir[0] + k_dir[1]) * r_k, axis=-1, keepdims=True) * hd(v)
    rwkv_out = (yn + bonus.reshape(b, l, R_WIDTH)) * g
    f_out = fourier_mix(f_in)
    gate_f, gate_r = jnp.split(gates, 2, axis=-1)
    merged = (jax.nn.sigmoid(gate_f) * (f_out @ w_fourier_up)
              + jax.nn.sigmoid(gate_r) * (rwkv_out @ w_rwkv_up))
    return merged @ w_out, s_final


def sq_relu_mlp(h, w1, w2):
    return jnp.square(jax.nn.relu(h @ w1)) @ w2


def setup_inputs(seed: int = 0) -> dict:
    key = jax.random.key(seed)
    ks = jax.random.split(key, 32)
    nrm = lambda k, shape, s: jax.random.normal(k, shape, jnp.float32) * s
    return {
        "x": nrm(ks[0], (BATCH, SEQ, D_MODEL), 1.0),
        "c": nrm(ks[1], (BATCH, D_MODEL), 1.0),
        "ctx": nrm(ks[2], (BATCH, CTX_LEN, D_MODEL), 1.0),
        "c_ctx": nrm(ks[3], (D_MODEL,), 1.0),
        "w_mod": nrm(ks[4], (DEPTH, D_MODEL, N_MOD * D_MODEL), 0.5 * D_MODEL ** -0.5),
        "b_mod": nrm(ks[5], (DEPTH, N_MOD * D_MODEL), 0.02),
        "norm1": 1.0 + nrm(ks[6], (DEPTH, D_MODEL), 0.05),
        "norm2": 1.0 + nrm(ks[7], (DEPTH, D_MODEL), 0.05),
        "w_in": nrm(ks[8], (DEPTH, D_MODEL, IN_WIDTH), D_MODEL ** -0.5),
        "mu_shift": jax.random.uniform(ks[9], (DEPTH, RWKV_IN), jnp.float32),
        "w0": jax.random.uniform(ks[10], (DEPTH, N_DIRS, R_WIDTH), jnp.float32, -5.0, -0.5),
        "w_up": nrm(ks[11], (DEPTH, N_DIRS, D_DECAY_LORA, R_WIDTH), 0.5 * D_DECAY_LORA ** -0.5),
        "a0": nrm(ks[12], (DEPTH, N_DIRS, R_WIDTH), 0.3),
        "a_up": nrm(ks[13], (DEPTH, N_DIRS, D_AAA_LORA, R_WIDTH), 0.5 * D_AAA_LORA ** -0.5),
        "g_up": nrm(ks[14], (DEPTH, D_GATE_LORA, R_WIDTH), D_GATE_LORA ** -0.5),
        "k_k": 0.85 + nrm(ks[15], (DEPTH, R_WIDTH), 0.05),
        "k_a": 1.0 + nrm(ks[16], (DEPTH, R_WIDTH), 0.05),
        "r_k": nrm(ks[17], (DEPTH, N_RHEADS, HEAD_DIM), 0.1),
        "ln_x_w": 1.0 + nrm(ks[18], (DEPTH, R_WIDTH), 0.05),
        "ln_x_b": nrm(ks[19], (DEPTH, R_WIDTH), 0.02),
        "w_fourier_up": nrm(ks[20], (DEPTH, F_WIDTH, D_MODEL), F_WIDTH ** -0.5),
        "w_rwkv_up": nrm(ks[21], (DEPTH, R_WIDTH, D_MODEL), R_WIDTH ** -0.5),
        "w_out": nrm(ks[22], (DEPTH, D_MODEL, D_MODEL), D_MODEL ** -0.5),
        "mlp_w1": nrm(ks[23], (DEPTH, D_MODEL, D_FF), D_MODEL ** -0.5),
        "mlp_w2": nrm(ks[24], (DEPTH, D_FF, D_MODEL), D_FF ** -0.5),
        "norm_f": 1.0 + nrm(ks[25], (D_MODEL,), 0.05),
    }


def reference(x, c, ctx, c_ctx, w_mod, b_mod, norm1, norm2, w_in, mu_shift, w0, w_up, a0,
              a_up, g_up, k_k, k_a, r_k, ln_x_w, ln_x_b, w_fourier_up, w_rwkv_up, w_out,
              mlp_w1, mlp_w2, norm_f):
    x_lat, x_ctx = x, ctx
    s0 = jnp.zeros((N_DIRS, x.shape[0], N_RHEADS, HEAD_DIM, HEAD_DIM), jnp.float32)
    for l in range(DEPTH):
        last = l == DEPTH - 1
        mod = jax.nn.silu(c) @ w_mod[l] + b_mod[l]
        sh1, sc1, g1, sh2, sc2, g2 = jnp.split(mod[:, None, :], N_MOD, axis=-1)
        mod_c = jax.nn.silu(c_ctx) @ w_mod[l] + b_mod[l]
        ch1, cs1, cg1, ch2, cs2, cg2 = jnp.split(mod_c, N_MOD, axis=-1)
        mixer = functools.partial(
            token_mixer, w_in=w_in[l], mu_shift=mu_shift[l], w0=w0[l], w_up=w_up[l],
            a0=a0[l], a_up=a_up[l], g_up=g_up[l], k_k=k_k[l], k_a=k_a[l], r_k=r_k[l],
            ln_x_w=ln_x_w[l], ln_x_b=ln_x_b[l], w_fourier_up=w_fourier_up[l],
            w_rwkv_up=w_rwkv_up[l], w_out=w_out[l])
        h_c = rms_norm(x_ctx, norm1[l]) * (1 + cs1) + ch1
        out_c, s_ctx = mixer(h_c, s0, seq_shift, not last)
        h = rms_norm(x_lat, norm1[l]) * (1 + sc1) + sh1
        out, _ = mixer(h, s_ctx, grid_shift, True)
        x_lat = x_lat + g1 * out
        h = rms_norm(x_lat, norm2[l]) * (1 + sc2) + sh2
        x_lat = x_lat + g2 * sq_relu_mlp(h, mlp_w1[l], mlp_w2[l])
        if not last:
            x_ctx = x_ctx + cg1 * out_c
            h_c = rms_norm(x_ctx, norm2[l]) * (1 + cs2) + ch2
            x_ctx = x_ctx + cg2 * sq_relu_mlp(h_c, mlp_w1[l], mlp_w2[l])
    return rms_norm(x_lat, norm_f)
```

```python
import numpy as np
import ml_dtypes
import concourse.bass as bass
import concourse.mybir as mybir
from concourse.bass_utils import run_bass_kernel_spmd
from contextlib import ExitStack

F32 = mybir.dt.float32
BF16 = mybir.dt.bfloat16
AF = mybir.ActivationFunctionType
ALU = mybir.AluOpType

ENGS = ("pe", "act", "dve", "pool", "sp")
DMA_SLOTS = 8
EPOCH = 30000

NL = 2
T = 2304
TCTX = 256
TLAT = 2048
TILES = [(0, 256), (256, 512), (768, 512), (1280, 512), (1792, 512)]
NEG_E = -0.6065306597126334


class Buf:
    __slots__ = ("name", "writers", "readers")

    def __init__(self, name=""):
        self.name = name
        self.writers = []
        self.readers = []


class TB:
    def __init__(self, ap, bufs):
        self.ap = ap
        self.bufs = bufs


class Op:
    __slots__ = ("eng", "fn", "dma", "deps", "signal", "ticket", "idx")

    def __init__(self, eng, fn, dma):
        self.eng = eng
        self.fn = fn
        self.dma = dma
        self.deps = []
        self.signal = dma
        self.ticket = None
        self.idx = None


def _bufs(lst):
    out = []
    for t in lst:
        if isinstance(t, Buf):
            out.append(t)
        else:
            out.extend(t.bufs)
    return out


class Prog:
    def __init__(self, nc):
        self.nc = nc
        self.ops = []
        self.stack = ExitStack()
        self.per_eng_ops = {e: [] for e in ENGS}
        self.pending_bar = {e: set() for e in ENGS}
        self.dmas_since_bar = []

    def sbuf(self, name, shape, dtype=F32):
        return self.stack.enter_context(self.nc.sbuf_tensor(name, list(shape), dtype))

    def psum(self, name, shape, dtype=F32):
        return self.stack.enter_context(self.nc.psum_tensor(name, list(shape), dtype))

    def _add(self, eng, fn, R, W, dma=False):
        reads = _bufs(R)
        writes = _bufs(W)
        op = Op(eng, fn, dma)
        op.idx = len(self.ops)
        deps = set()
        for b in reads:
            deps.update(b.writers)
        for b in writes:
            deps.update(b.writers)
            deps.update(b.readers)
        if self.pending_bar[eng]:
            deps.update(self.pending_bar[eng])
            self.pending_bar[eng] = set()
        op.deps = deps
        if dma:
            self.dmas_since_bar.append(op.idx)
        for b in reads:
            b.readers.append(op.idx)
        for b in writes:
            b.writers = [op.idx]
            b.readers = []
        self.ops.append(op)
        self.per_eng_ops[eng].append(op)
        return op

    def barrier(self):
        deps = set(self.dmas_since_bar)
        for e in ENGS:
            comp = [op.idx for op in self.per_eng_ops[e] if not op.dma]
            if comp:
                deps.add(comp[-1])
        for e in ENGS:
            self.pending_bar[e] = set(deps) | self.pending_bar[e]
        self.dmas_since_bar = []

    def mm(self, out, lhsT, rhs, start, stop, R, W):
        self._add("pe", lambda e: e.matmul(out, lhsT=lhsT, rhs=rhs, start=start, stop=stop), R, W)

    def actf(self, out, in_, func, R, W, bias=None, scale=None):
        kw = {}
        if bias is not None:
            kw["bias"] = bias
        if scale is not None:
            kw["scale"] = scale
        self._add("act", lambda e: e.activation(out=out, in_=in_, func=func, **kw), R, W)

    def tt(self, eng, out, in0, in1, op, R, W):
        self._add(eng, lambda e: e.tensor_tensor(out=out, in0=in0, in1=in1, op=op), R, W)

    def ts(self, eng, out, in0, s1, s2, op0, op1, R, W):
        if s2 is None:
            self._add(eng, lambda e: e.tensor_scalar(out=out, in0=in0, scalar1=s1, scalar2=None, op0=op0), R, W)
        else:
            self._add(eng, lambda e: e.tensor_scalar(out=out, in0=in0, scalar1=s1, scalar2=s2, op0=op0, op1=op1), R, W)

    def stt(self, out, in0, scalar, in1, op0, op1, R, W):
        self._add("dve", lambda e: e.scalar_tensor_tensor(out=out, in0=in0, scalar=scalar, in1=in1, op0=op0, op1=op1), R, W)

    def cp(self, eng, out, in_, R, W):
        if eng == "act":
            self._add("act", lambda e: e.copy(out=out, in_=in_), R, W)
        else:
            self._add(eng, lambda e: e.tensor_copy(out=out, in_=in_), R, W)

    def memset(self, eng, ap, val, W):
        self._add(eng, lambda e: e.memset(ap, val), [], W)

    def scan(self, out, d0, d1, R, W):
        self._add("dve", lambda e: e.tensor_tensor_scan(out=out, data0=d0, data1=d1, initial=0.0, op0=ALU.mult, op1=ALU.add), R, W)

    def recip(self, out, in_, R, W):
        self._add("dve", lambda e: e.reciprocal(out=out, in_=in_), R, W)

    def dma(self, q, out, in_, R, W):
        self._add(q, lambda e: e.dma_start(out=out, in_=in_), R, W, dma=True)

    def emit(self):
        nc = self.nc
        ops = self.ops
        pos = {}
        for e in ENGS:
            for i, op in enumerate(self.per_eng_ops[e]):
                pos[op.idx] = i
        for op in ops:
            best = {}
            dmas = []
            for d in op.deps:
                p = ops[d]
                if p.dma:
                    dmas.append(d)
                    continue
                if p.eng == op.eng and not op.dma:
                    if p.eng == "pe":
                        continue
                if p.eng not in best or pos[d] > pos[best[p.eng]]:
                    best[p.eng] = d
            op.deps = sorted(dmas) + sorted(best.values())
            for d in op.deps:
                ops[d].signal = True
        sems = {}

        def getsem(key):
            if key not in sems:
                sems[key] = self.stack.enter_context(nc.semaphore("s_%s_%s" % key))
            return sems[key]

        cnt = {e: 0 for e in ENGS}
        dcnt = {e: 0 for e in ENGS}
        for e in ENGS:
            for op in self.per_eng_ops[e]:
                if op.dma:
                    j = dcnt[e]
                    dcnt[e] += 1
                    op.ticket = (getsem((e + "q", j % DMA_SLOTS)), 16 * (j // DMA_SLOTS + 1))
                elif op.signal:
                    j = cnt[e]
                    cnt[e] += 1
                    op.ticket = (getsem((e, j // EPOCH)), (j % EPOCH) + 1)
        for e in ENGS:
            dl = [op for op in self.per_eng_ops[e] if op.dma]
            for j, op in enumerate(dl):
                if j >= DMA_SLOTS:
                    op.deps = [dl[j - DMA_SLOTS].idx] + [d for d in op.deps if d != dl[j - DMA_SLOTS].idx]
        with nc.Block() as block:
            def make(ename):
                def body(engobj):
                    seen = {}
                    for op in self.per_eng_ops[ename]:
                        need = {}
                        for d in op.deps:
                            sem, val = ops[d].ticket
                            k = id(sem)
                            if seen.get(k, 0) >= val:
                                continue
                            if k not in need or need[k][1] < val:
                                need[k] = (sem, val)
                        for k, (sem, val) in need.items():
                            engobj.wait_ge(sem, val)
                            seen[k] = val
                        ins = op.fn(engobj)
                        if op.ticket is not None:
                            ins.then_inc(op.ticket[0], 16 if op.dma else 1)
                    last = {}
                    for op in self.per_eng_ops[ename]:
                        if op.dma:
                            last[id(op.ticket[0])] = op.ticket
                    for sem, val in last.values():
                        if seen.get(id(sem), 0) < val:
                            engobj.wait_ge(sem, val)
                return body

            if self.per_eng_ops["sp"]:
                block.sync(make("sp"))
            if self.per_eng_ops["act"]:
                block.scalar(make("act"))
            if self.per_eng_ops["dve"]:
                block.vector(make("dve"))
            if self.per_eng_ops["pool"]:
                block.gpsimd(make("pool"))
            if self.per_eng_ops["pe"]:
                block.tensor(make("pe"))
        self.stack.close()


class Arena:
    def __init__(self, P, words):
        self.t = P.sbuf("arena", [128, words], F32)
        self.words = words
        self.off = 0
        self.n = 0

    def reset(self, to=0):
        self.off = to

    def f32(self, n, name="t"):
        ap = self.t[:, self.off:self.off + n]
        self.off += n
        assert self.off <= self.words, "arena overflow %d > %d" % (self.off, self.words)
        self.n += 1
        return TB(ap, [Buf("%s%d" % (name, self.n))])

    def bf16(self, n, name="t"):
        w = (n + 1) // 2
        ap = self.t[:, self.off:self.off + w].bitcast(BF16)
        self.off += w
        assert self.off <= self.words, "arena overflow %d > %d" % (self.off, self.words)
        self.n += 1
        return TB(ap, [Buf("%s%d" % (name, self.n))])


class Rot:
    def __init__(self, items):
        self.items = items
        self.i = 0

    def next(self):
        it = self.items[self.i % len(self.items)]
        self.i += 1
        return it


def build_program(nl=NL, dbg=False, stop_after=None):
    nc = bass.Bass("TRN2", target_bir_lowering=False)
    P = Prog(nc)

    def din(name, shape, dt=F32):
        return nc.dram_tensor(name, list(shape), dt, kind="ExternalInput").ap()

    def dscr(name, shape, dt=F32):
        return nc.dram_tensor(name, list(shape), dt, kind=("ExternalOutput" if dbg else "Internal")).ap()

    xT_d = din("xT", [128, 8, T])
    cvec_d = din("cvec", [128, 16])
    wmod_d = din("w_mod", [NL, 1024, 6144])
    bmod_d = din("bmod", [NL, 128, 48])
    norms_d = din("norms", [128, 40])
    win_d = din("w_in", [NL, 1024, 4480])
    mu_d = din("mu", [NL, 128, 15])
    pv_d = din("pv", [NL, 128, 36])
    wup_d = din("wup", [NL, 2, 128, 512])
    aup_d = din("aup", [NL, 2, 128, 512])
    gup_d = din("gup", [NL, 128, 512])
    wf_d = din("w_f", [NL, 512, 1024])
    wr_d = din("w_r", [NL, 512, 1024])
    wo_d = din("w_o", [NL, 1024, 1024])
    w1_d = din("w1", [NL, 1024, 4096])
    w2_d = din("w2", [NL, 4096, 1024])
    ident_d = din("ident", [128, 128])
    onesblk_d = din("onesblk", [128, 128])
    masks_d = din("masks", [128, 4 * 512])
    ident8_d = din("ident8", [128, 512])
    dftc_d = din("dftc", [128, 256], BF16)
    cosL_d = din("cosL", [2048, 2048], BF16)
    sinL_d = din("sinL", [2048, 2048], BF16)
    cosC_d = din("cosC", [256, 256], BF16)
    sinC_d = din("sinC", [256, 256], BF16)
    out_d = nc.dram_tensor("outT", [128, 8, TLAT], F32, kind="ExternalOutput").ap()

    xS = dscr("xS", [128, 8, T])
    rwS = dscr("rwS", [15, 128, T])
    finS = dscr("finS", [4, 128, T], BF16)
    gateS = dscr("gateS", [16, 128, T], BF16)
    yS = dscr("yS", [2, 128, 4, T])
    foutS = dscr("foutS", [4, 128, T], BF16)
    if dbg:
        dbg_rwk = dscr("dbg_rwk", [128, 4, T], BF16)
        dbg_mg = dscr("dbg_mg", [128, 8, T], BF16)
        dbg_xmid = dscr("dbg_xmid", [128, 8, T])
    wfB = nc.dram_tensor("wfB", [8, 128, 512], BF16, kind="Internal").ap()
    wrB = nc.dram_tensor("wrB", [8, 128, 512], BF16, kind="Internal").ap()
    woB = nc.dram_tensor("woB", [8, 128, 1024], BF16, kind="Internal").ap()
    w1B = nc.dram_tensor("w1B", [32, 128, 1024], BF16, kind="Internal").ap()
    w2B = nc.dram_tensor("w2B", [32, 128, 1024], BF16, kind="Internal").ap()
    wfB_b = [Buf("wfB%d" % i) for i in range(8)]
    wrB_b = [Buf("wrB%d" % i) for i in range(8)]
    woB_b = [Buf("woB%d" % i) for i in range(8)]
    w1B_b = [Buf("w1B%d" % i) for i in range(32)]
    w2B_b = [Buf("w2B%d" % i) for i in range(32)]
    xS_b = [Buf("xS%d" % i) for i in range(5)]
    rwS_b = [Buf("rwS%d" % i) for i in range(15)]
    finS_b = [Buf("finS%d" % i) for i in range(4)]
    gateS_b = [Buf("gateS%d" % i) for i in range(16)]
    yS_b = [[Buf("yS%d_%d" % (d, c)) for c in range(36)] for d in range(2)]
    foutS_b = [[Buf("foutS%d_%d" % (g, i)) for i in range(5)] for g in range(4)]

    def ptile(name, shape, dt=F32):
        t = P.sbuf(name + "_sb", shape, dt)
        return TB(t[:], [Buf(name)])

    ident = ptile("ident", [128, 128])
    onesblk = ptile("onesblk", [128, 128])
    onesblkm = ptile("onesblkm", [128, 128])
    ones128 = ptile("ones128", [128, 128])
    masks = ptile("masks", [128, 2048])
    ident8 = ptile("ident8", [128, 512])
    dftc = ptile("dftc", [128, 256], BF16)
    eps_n = ptile("eps_n", [128, 1])
    eps_g = ptile("eps_g", [128, 1])
    sc = ptile("sc", [128, 16])
    norms = ptile("norms", [128, 40])
    P.dma("sp", ident.ap, ident_d, [], [ident])
    P.dma("sp", onesblk.ap, onesblk_d, [], [onesblk])
    P.dma("sp", masks.ap, masks_d, [], [masks])
    P.dma("sp", ident8.ap, ident8_d, [], [ident8])
    P.dma("sp", dftc.ap, dftc_d, [], [dftc])
    P.dma("sp", sc.ap, cvec_d, [], [sc])
    P.dma("sp", norms.ap, norms_d, [], [norms])
    P.memset("pool", ones128.ap, 1.0, [ones128])
    P.memset("pool", eps_n.ap, 1e-6, [eps_n])
    P.memset("pool", eps_g.ap, 64e-5, [eps_g])
    P.ts("pool", onesblkm.ap, onesblk.ap, 1.0 / 64.0, None, ALU.mult, None, [onesblk], [onesblkm])
    P.actf(sc.ap, sc.ap, AF.Silu, [sc], [sc])
    identb = ptile("identb", [128, 128], BF16)
    P.cp("dve", identb.ap, ident.ap, [ident], [identb])

    mod, cA, mu, omm, pv, omka, c2, wup, aup, gup, bm = [], [], [], [], [], [], [], [], [], [], []
    for l in range(nl):
        mod.append(ptile("mod%d" % l, [128, 96]))
        cA.append(ptile("cA%d" % l, [128, 32]))
        mu.append(ptile("mu%d" % l, [128, 15]))
        omm.append(ptile("omm%d" % l, [128, 15]))
        pv.append(ptile("pv%d" % l, [128, 36]))
        omka.append(ptile("omka%d" % l, [128, 4]))
        c2.append(ptile("c2%d" % l, [128, 4]))
        wup.append([ptile("wup%d_%d" % (l, d), [128, 512]) for d in range(2)])
        aup.append([ptile("aup%d_%d" % (l, d), [128, 512]) for d in range(2)])
        gup.append(ptile("gup%d" % l, [128, 512]))
        bm.append(ptile("bm%d" % l, [128, 48]))
        P.dma("sp", mu[l].ap, mu_d[l], [], [mu[l]])
        P.dma("sp", pv[l].ap, pv_d[l], [], [pv[l]])
        P.dma("sp", bm[l].ap, bmod_d[l], [], [bm[l]])
        P.dma("sp", gup[l].ap, gup_d[l], [], [gup[l]])
        for d in range(2):
            P.dma("sp", wup[l][d].ap, wup_d[l, d], [], [wup[l][d]])
            P.dma("sp", aup[l][d].ap, aup_d[l, d], [], [aup[l][d]])
        P.ts("pool", omm[l].ap, mu[l].ap, -1.0, 1.0, ALU.mult, ALU.add, [mu[l]], [omm[l]])
        P.ts("pool", omka[l].ap, pv[l].ap[:, 20:24], -1.0, 1.0, ALU.mult, ALU.add, [pv[l]], [omka[l]])
        P.ts("pool", c2[l].ap, pv[l].ap[:, 20:24], -2.0, 2.0, ALU.mult, ALU.add, [pv[l]], [c2[l]])

    pst = [P.psum("ps%d" % i, [128, 1024]) for i in range(4)]
    banks = []
    for i in range(4):
        for hlf in range(2):
            banks.append(TB(pst[i][:, hlf * 512:(hlf + 1) * 512], [Buf("bank%d" % (2 * i + hlf))]))
    PS = Rot(banks)

    AW = (nc.sbuf_bytes_remaining - 1024) // 4
    A = Arena(P, AW)

    wm_slots = [A.f32(4096, "wm") for _ in range(2)]
    modrow = A.f32(6144, "modrow")
    for l in range(nl):
        psb = PS.next()
        for g in range(12):
            wm = wm_slots[g % 2]
            src = wmod_d[l].rearrange("(k p) n -> p k n", p=128)[:, :, g * 512:(g + 1) * 512]
            P.dma("sp" if g % 2 == 0 else "pool", wm.ap.rearrange("p (k n) -> p k n", k=8), src, [], [wm])
            wm3 = wm.ap.rearrange("p (k n) -> p k n", k=8)
            pr = PS.next()
            for k in range(8):
                P.mm(pr.ap[0:2, 0:512], sc.ap[:, 2 * k:2 * k + 2], wm3[:, k, :], k == 0, k == 7, [wm, sc], [pr])
            P.cp("act", modrow.ap[0:2, g * 512:(g + 1) * 512], pr.ap[0:2, 0:512], [pr], [modrow])
        for jj in range(48):
            P.mm(psb.ap[:, 2 * jj:2 * jj + 2], modrow.ap[0:2, jj * 128:(jj + 1) * 128], ident.ap[0:2, 0:2], True, True,
                 [modrow, ident], [psb])
        P.tt("dve", mod[l].ap.rearrange("p (j c) -> p j c", c=2), psb.ap[:, 0:96].rearrange("p (j c) -> p j c", c=2),
             bm[l].ap.rearrange("p (j o) -> p j o", o=1).broadcast_to([128, 48, 2]), ALU.add, [psb, bm[l]], [mod[l]])
        m3 = mod[l].ap.rearrange("p (j c) -> p j c", c=2)
        cA3 = cA[l].ap.rearrange("p (w k c) -> p w k c", w=2, k=8)
        for which, (sec, n0) in enumerate(((8, l * 8), (32, 16 + l * 8))):
            for col in range(2):
                P.stt(cA3[:, which, :, col], m3[:, sec:sec + 8, col], 1.0, norms.ap[:, n0:n0 + 8], ALU.add, ALU.mult,
                      [mod[l], norms], [cA[l]])

    def modcol(l, sec, k, col):
        j = ((sec * 8 + k) * 2 + col)
        return mod[l].ap[:, j:j + 1]

    def cAcol(l, which, k, col):
        j = (which * 8 + k) * 2 + col
        return cA[l].ap[:, j:j + 1]

    class WL:
        def __init__(self, ns=3):
            self.st = [A.f32(1024, "wst") for _ in range(ns)]
            self.bf = [A.bf16(1024, "wbf") for _ in range(ns)]
            self.i = 0

        def load(self, src3, kc, parts=128):
            st = self.st[self.i % len(self.st)]
            bf = self.bf[self.i % len(self.bf)]
            n = kc * 128
            P.dma("sp", st.ap[0:parts, 0:n].rearrange("p (k n) -> p k n", k=kc), src3, [], [st])
            eng = "act" if self.i % 2 == 0 else "dve"
            P.cp(eng, bf.ap[0:parts, 0:n], st.ap[0:parts, 0:n], [st], [bf])
            self.i += 1
            return bf

    class BL:
        def __init__(self, ns=4):
            self.bf = [A.bf16(1024, "wbl") for _ in range(ns)]
            self.i = 0

        def load(self, src2, n, dep):
            bf = self.bf[self.i % len(self.bf)]
            self.i += 1
            P.dma("sp", bf.ap[:, 0:n], src2, [dep], [bf])
            return bf

    def wview(w_l, c0, k0=0, kc=8):
        return w_l.rearrange("(k p) n -> p k n", p=128)[:, k0:k0 + kc, c0:c0 + 128]

    def prefetched(wl, specs, ahead=2):
        q = []
        it = iter(specs)
        for _ in range(ahead):
            s_ = next(it, None)
            if s_ is not None:
                q.append(wl.load(*s_))
        for _ in range(len(specs)):
            s_ = next(it, None)
            if s_ is not None:
                q.append(wl.load(*s_))
            yield q.pop(0)

    def rms_rstd(x3, nt, sq, rstd, xtb):
        sq3 = sq.ap[:, 0:8 * nt].rearrange("p (k t) -> p k t", k=8)
        P.actf(sq3, x3, AF.Square, [xtb], [sq])
        ps = PS.next()
        for k in range(8):
            P.mm(ps.ap[:, 0:nt], ones128.ap, sq3[:, k, :], k == 0, k == 7, [ones128, sq], [ps])
        P.actf(rstd.ap[:, 0:nt], ps.ap[:, 0:nt], AF.Sqrt, [ps, eps_n], [rstd], bias=eps_n.ap, scale=1.0 / 1024.0)
        P.recip(rstd.ap[:, 0:nt], rstd.ap[:, 0:nt], [rstd], [rstd])

    def norm_mod(l, which, sec_b, col, x3, nt, sq, rstd, tmps, xtb, out3, outtb):
        rms_rstd(x3, nt, sq, rstd, xtb)
        for k in range(8):
            tmp = tmps[k % 2]
            P.stt(tmp.ap[:, 0:nt], x3[:, k, :], cAcol(l, which, k, col), rstd.ap[:, 0:nt], ALU.mult, ALU.mult,
                  [xtb, cA[l], rstd], [tmp])
            P.actf(out3[:, k, :], tmp.ap[:, 0:nt], AF.Identity, [tmp, mod[l]], [outtb], bias=modcol(l, sec_b, k, col))

    arena_base = 0
    A.reset(arena_base)

    for l in range(nl):
        last = (l == NL - 1)
        xsrc = xT_d if l == 0 else xS
        P.barrier()
        A.reset(arena_base)
        hT = A.bf16(8 * T, "hT")
        hT3 = hT.ap.rearrange("p (k t) -> p k t", k=8)
        mark = A.off
        xts = [A.f32(4096, "xt") for _ in range(2)]
        sq = A.f32(4096, "sq")
        rstd = A.f32(512, "rstd")
        tmp = [A.f32(512, "tmp") for _ in range(2)]
        for ti, (t0, nt) in enumerate(TILES):
            xt = xts[ti % 2]
            x3 = xt.ap[:, 0:8 * nt].rearrange("p (k t) -> p k t", k=8)
            P.dma("sp", x3, xsrc[:, :, t0:t0 + nt], [xS_b[ti]] if l > 0 else [], [xt])
            col = 1 if ti == 0 else 0
            norm_mod(l, 0, 0, col, x3, nt, sq, rstd, tmp, xt, hT3[:, :, t0:t0 + nt], hT)
        P.barrier()
        A.reset(mark)
        wl = WL(3)
        finrows = [A.bf16(T, "finrow") for _ in range(2)]
        rawrows = [A.f32(T, "rawrow") for _ in range(2)]
        outrows = [A.f32(T, "outrow") for _ in range(2)]
        grows = [A.bf16(T, "grow") for _ in range(2)]
        specs = [(wview(win_d[l], fc * 128), 8) for fc in range(35)]
        for fc, wb in enumerate(prefetched(wl, specs)):
            wb3 = wb.ap.rearrange("p (k n) -> p k n", k=8)
            if fc < 4:
                row = finrows[fc % 2]
            elif fc < 19:
                row = rawrows[fc % 2]
            else:
                row = grows[fc % 2]
            for ti, (t0, nt) in enumerate(TILES):
                ps = PS.next()
                for k in range(8):
                    P.mm(ps.ap[:, 0:nt], wb3[:, k, :], hT3[:, k, t0:t0 + nt], k == 0, k == 7, [wb, hT], [ps])
                if fc < 19:
                    P.cp("act", row.ap[:, t0:t0 + nt], ps.ap[:, 0:nt], [ps], [row])
                else:
                    P.actf(row.ap[:, t0:t0 + nt], ps.ap[:, 0:nt], AF.Sigmoid, [ps], [row])
            if fc < 4:
                P.dma("pool", finS[fc], row.ap, [row], [finS_b[fc]])
            elif fc >= 19:
                P.dma("pool", gateS[fc - 19], row.ap, [row], [gateS_b[fc - 19]])
            else:
                j = fc - 4
                orow = outrows[fc % 2]
                P.actf(orow.ap, row.ap, AF.Identity, [row, omm[l]], [orow], scale=omm[l].ap[:, j:j + 1])
                p = 0
                while p < 128:
                    ch = 128 * j + p
                    q = ch // 480
                    hq = ch // 960
                    nb = min((q + 1) * 480, (hq + 1) * 960, 128 * (j + 1)) - 128 * j
                    if p == 32:
                        nb = min(nb, 64)
                    elif p == 96:
                        nb = min(nb, 128)
                    muc = mu[l].ap[p:nb, j:j + 1]
                    R3 = row.ap[p:nb, 256:T].rearrange("p (r c) -> p r c", c=64)
                    O3 = orow.ap[p:nb, 256:T].rearrange("p (r c) -> p r c", c=64)
                    if q == 0:
                        o_, i_ = O3[:, :, 1:64], R3[:, :, 0:63]
                    elif q == 1:
                        o_, i_ = O3[:, :, 0:63], R3[:, :, 1:64]
                    elif q == 2:
                        o_, i_ = O3[:, 1:32, :], R3[:, 0:31, :]
                    else:
                        o_, i_ = O3[:, 0:31, :], R3[:, 1:32, :]
                    P.stt(o_, i_, muc, o_, ALU.mult, ALU.add, [row, mu[l], orow], [orow])
                    Rc = row.ap[p:nb, 0:256]
                    Oc = orow.ap[p:nb, 0:256]
                    if hq == 0:
                        o_, i_ = Oc[:, 1:256], Rc[:, 0:255]
                    else:
                        o_, i_ = Oc[:, 0:255], Rc[:, 1:256]
                    P.stt(o_, i_, muc, o_, ALU.mult, ALU.add, [row, mu[l], orow], [orow])
                    p = nb
                P.dma("pool", rwS[j], orow.ap, [orow], [rwS_b[j]])
        if stop_after == "p1":
            break

        P.barrier()
        A.reset(arena_base)
        finT = A.bf16(4 * T, "finT")
        fin3 = finT.ap.rearrange("p (g t) -> p g t", g=4)
        for g in range(4):
            P.dma("sp", fin3[:, g, :], finS[g], [finS_b[g]], [finT])
        Gcs = [A.bf16(1024, "Gcs") for _ in range(18)]
        for tcg in range(18):
            if last and tcg < 2:
                continue
            psa, psb2 = PS.next(), PS.next()
            for g in range(4):
                pp = psa if g < 2 else psb2
                P.mm(pp.ap[:, (g % 2) * 256:(g % 2) * 256 + 256], fin3[:, g, tcg * 128:(tcg + 1) * 128], dftc.ap,
                     True, True, [finT, dftc], [pp])
            P.cp("act", Gcs[tcg].ap[:, 0:512], psa.ap, [psa], [Gcs[tcg]])
            P.cp("dve", Gcs[tcg].ap[:, 512:1024], psb2.ap, [psb2], [Gcs[tcg]])
        ctabs = [(A.bf16(16 * 512, "cosT"), A.bf16(16 * 512, "sinT")) for _ in range(2)]
        ftiles = [A.bf16(512, "ftile") for _ in range(3)]
        fti = 0
        if not last:
            ct, stt_ = ctabs[0]
            c3 = ct.ap[:, 0:512].rearrange("p (k s) -> p k s", k=2)
            s3 = stt_.ap[:, 0:512].rearrange("p (k s) -> p k s", k=2)
            P.dma("sp", c3, cosC_d.rearrange("(k p) s -> p k s", p=128), [], [ct])
            P.dma("sp", s3, sinC_d.rearrange("(k p) s -> p k s", p=128), [], [stt_])
            for g in range(4):
                ps = PS.next()
                for tc in range(2):
                    P.mm(ps.ap[:, 0:256], Gcs[tc].ap[:, g * 256:g * 256 + 128], c3[:, tc, :], tc == 0, False,
                         [Gcs[tc], ct], [ps])
                    P.mm(ps.ap[:, 0:256], Gcs[tc].ap[:, g * 256 + 128:g * 256 + 256], s3[:, tc, :], False, tc == 1,
                         [Gcs[tc], stt_], [ps])
                ft = ftiles[fti % 3]
                fti += 1
                P.cp("act", ft.ap[:, 0:256], ps.ap[:, 0:256], [ps], [ft])
                P.dma("pool", foutS[g][:, 0:256], ft.ap[:, 0:256], [ft], [foutS_b[g][0]])
        for st in range(4):
            ct, stt_ = ctabs[(st + 1) % 2]
            c3 = ct.ap.rearrange("p (k s) -> p k s", k=16)
            s3 = stt_.ap.rearrange("p (k s) -> p k s", k=16)
            cv = cosL_d.rearrange("(k p) s -> p k s", p=128)
            sv = sinL_d.rearrange("(k p) s -> p k s", p=128)
            for hh in range(2):
                P.dma("sp", c3[:, hh * 8:(hh + 1) * 8, :], cv[:, hh * 8:(hh + 1) * 8, st * 512:(st + 1) * 512], [], [ct])
                P.dma("pool", s3[:, hh * 8:(hh + 1) * 8, :], sv[:, hh * 8:(hh + 1) * 8, st * 512:(st + 1) * 512], [], [stt_])
            for g in range(4):
                ps = PS.next()
                for tc in range(16):
                    P.mm(ps.ap, Gcs[2 + tc].ap[:, g * 256:g * 256 + 128], c3[:, tc, :], tc == 0, False,
                         [Gcs[2 + tc], ct], [ps])
                    P.mm(ps.ap, Gcs[2 + tc].ap[:, g * 256 + 128:g * 256 + 256], s3[:, tc, :], False, tc == 15,
                         [Gcs[2 + tc], stt_], [ps])
                ft = ftiles[fti % 3]
                fti += 1
                P.cp("act", ft.ap, ps.ap, [ps], [ft])
                P.dma("pool", foutS[g][:, 256 + st * 512:256 + (st + 1) * 512], ft.ap, [ft], [foutS_b[g][1 + st]])
        if stop_after == "p3":
            break

        P.barrier()
        A.reset(arena_base)
        PSS = Rot(banks[2:8])

        def v3(tb):
            return tb.ap.rearrange("p (g i) -> p g i", i=64)

        def scan_dir(d):
            QN = ("RhT", "KhT", "kapT", "bhT", "btT", "ktT", "vT", "Dg")
            ops_ = {q: [A.bf16(256, q) for _ in range(4)] for q in QN}
            tw = A.f32(256, "tw")
            adu = A.f32(256, "adu")
            tn = ("r", "k", "v", "sig", "lw", "a", "kk", "sq", "nrm", "kap", "t1", "kdir", "beta", "cum", "L", "Lend", "Lprev",
                  "eL", "enL", "ePrev", "eEnd", "onesr")
            tp = {n: A.f32(256, n) for n in tn}
            gC = A.f32(4, "gC")
            P.memset("pool", tp["onesr"].ap, 1.0, [tp["onesr"]])
            pn = ("N", "NT", "N2", "NT2", "X", "AkkT", "BrkT", "Vtok", "kttok", "QT", "PhiT")
            pt = {n: A.bf16(512, n) for n in pn}
            for n in ("Y0T", "Psi", "Xf"):
                pt[n] = A.f32(512, n)
            W2 = A.bf16(1024, "W2")
            RHS1 = A.bf16(1024, "RHS1")
            negPU = A.bf16(1024, "negPU")
            Hs = [A.bf16(256, "H") for _ in range(2)]
            youts = [A.f32(256, "yout") for _ in range(2)]
            W2v = W2.ap.rearrange("p (g x) -> p g x", x=128)
            RHS1v = RHS1.ap.rearrange("p (g x) -> p g x", x=128)

            mT_s = masks.ap[:, (0 if d == 0 else 1) * 512:(0 if d == 0 else 1) * 512 + 512]
            mT_i = masks.ap[:, (2 if d == 0 else 3) * 512:(2 if d == 0 else 3) * 512 + 512]
            m_s = masks.ap[:, (1 if d == 0 else 0) * 512:(1 if d == 0 else 0) * 512 + 512]
            hi = 0
            P.memset("pool", Hs[0].ap, 0.0, [Hs[0]])
            units = list(range(9)) if d == 0 else [0] + list(range(8, 0, -1))
            for u in units:
                t0 = u * 256
                P.dma("sp", tw.ap, rwS[12][:, t0:t0 + 256], [rwS_b[12]], [tw])
                P.dma("sp", adu.ap, rwS[13][:, t0:t0 + 256], [rwS_b[13]], [adu])
                P.actf(tw.ap, tw.ap, AF.Tanh, [tw], [tw])
                for hp in range(4):
                    o = {q: ops_[q][hp] for q in QN}
                    P.dma("sp", tp["r"].ap, rwS[hp][:, t0:t0 + 256], [rwS_b[hp]], [tp["r"]])
                    P.dma("sp", tp["k"].ap, rwS[4 + hp][:, t0:t0 + 256], [rwS_b[4 + hp]], [tp["k"]])
                    P.dma("sp", tp["v"].ap, rwS[8 + hp][:, t0:t0 + 256], [rwS_b[8 + hp]], [tp["v"]])
                    P.cp("act", o["vT"].ap, tp["v"].ap, [tp["v"]], [o["vT"]])
                    pw = banks[d]
                    pb = PSS.next()
                    P.mm(pw.ap[:, 0:256], wup[l][d].ap[:, hp * 128:(hp + 1) * 128], tw.ap, True, True, [wup[l][d], tw], [pw])
                    P.mm(pw.ap[:, 256:512], aup[l][d].ap[:, hp * 128:(hp + 1) * 128], adu.ap, True, True, [aup[l][d], adu], [pw])
                    P.actf(tp["sig"].ap, pw.ap[:, 0:256], AF.Sigmoid, [pw, pv[l]], [tp["sig"]], bias=pv[l].ap[:, d * 4 + hp:d * 4 + hp + 1])
                    P.actf(tp["a"].ap, pw.ap[:, 256:512], AF.Sigmoid, [pw, pv[l]], [tp["a"]], bias=pv[l].ap[:, 8 + d * 4 + hp:8 + d * 4 + hp + 1])
                    P.ts("pool", tp["lw"].ap, tp["sig"].ap, NEG_E, None, ALU.mult, None, [tp["sig"]], [tp["lw"]])
                    P.ts("pool", tp["kk"].ap, tp["k"].ap, pv[l].ap[:, 16 + hp:17 + hp], None, ALU.mult, None, [tp["k"], pv[l]], [tp["kk"]])
                    P.tt("pool", tp["sq"].ap, tp["kk"].ap, tp["kk"].ap, ALU.mult, [tp["kk"]], [tp["sq"]])
                    P.mm(pb.ap[:, 0:256], onesblk.ap, tp["sq"].ap, True, True, [onesblk, tp["sq"]], [pb])
                    P.actf(tp["nrm"].ap, pb.ap[:, 0:256], AF.Sqrt, [pb], [tp["nrm"]])
                    P.ts("dve", tp["nrm"].ap, tp["nrm"].ap, 1e-12, None, ALU.max, None, [tp["nrm"]], [tp["nrm"]])
                    P.recip(tp["nrm"].ap, tp["nrm"].ap, [tp["nrm"]], [tp["nrm"]])
                    yield
                    P.tt("dve", tp["kap"].ap, tp["kk"].ap, tp["nrm"].ap, ALU.mult, [tp["kk"], tp["nrm"]], [tp["kap"]])
                    P.ts("pool", tp["t1"].ap, tp["a"].ap, pv[l].ap[:, 20 + hp:21 + hp], omka[l].ap[:, hp:hp + 1], ALU.mult, ALU.add,
                         [tp["a"], pv[l], omka[l]], [tp["t1"]])
                    P.tt("pool", tp["kdir"].ap, tp["k"].ap, tp["t1"].ap, ALU.mult, [tp["k"], tp["t1"]], [tp["kdir"]])
                    P.tt("dve", tp["beta"].ap, tp["kap"].ap, tp["a"].ap, ALU.mult, [tp["kap"], tp["a"]], [tp["beta"]])
                    P.scan(tp["cum"].ap, tp["onesr"].ap, tp["lw"].ap, [tp["onesr"], tp["lw"]], [tp["cum"]])
                    yield
                    cum3, L3, lw3 = v3(tp["cum"]), v3(tp["L"]), v3(tp["lw"])
                    if d == 0:
                        P.cp("pool", tp["L"].ap[:, 0:64], tp["cum"].ap[:, 0:64], [tp["cum"]], [tp["L"]])
                        P.tt("dve", L3[:, 1:4, :], cum3[:, 1:4, :], cum3[:, 0:3, 63:64].broadcast_to([128, 3, 64]), ALU.subtract,
                             [tp["cum"]], [tp["L"]])
                        Ltot = L3[:, :, 63:64]
                    else:
                        P.tt("pool", tp["L"].ap, tp["lw"].ap, tp["cum"].ap, ALU.subtract, [tp["lw"], tp["cum"]], [tp["L"]])
                        P.tt("dve", L3, L3, cum3[:, :, 63:64].broadcast_to([128, 4, 64]), ALU.add, [tp["L"], tp["cum"]], [tp["L"]])
                        Ltot = L3[:, :, 0:1]
                    P.tt("dve", v3(tp["Lend"]), Ltot.broadcast_to([128, 4, 64]), L3, ALU.subtract, [tp["L"]], [tp["Lend"]])
                    P.tt("pool", tp["Lprev"].ap, tp["L"].ap, tp["lw"].ap, ALU.subtract, [tp["L"], tp["lw"]], [tp["Lprev"]])
                    P.actf(tp["eL"].ap, tp["L"].ap, AF.Exp, [tp["L"]], [tp["eL"]])
                    P.actf(tp["enL"].ap, tp["L"].ap, AF.Exp, [tp["L"]], [tp["enL"]], scale=-1.0)
                    P.actf(tp["ePrev"].ap, tp["Lprev"].ap, AF.Exp, [tp["Lprev"]], [tp["ePrev"]])
                    P.actf(tp["eEnd"].ap, tp["Lend"].ap, AF.Exp, [tp["Lend"]], [tp["eEnd"]])
                    P.actf(gC.ap.rearrange("p (c o) -> p c o", o=1), Ltot, AF.Exp, [tp["L"]], [gC])
                    yield
                    P.tt("dve", o["RhT"].ap, tp["r"].ap, tp["eL"].ap, ALU.mult, [tp["r"], tp["eL"]], [o["RhT"]])
                    P.tt("pool", o["KhT"].ap, tp["kdir"].ap, tp["enL"].ap, ALU.mult, [tp["kdir"], tp["enL"]], [o["KhT"]])
                    P.tt("dve", o["kapT"].ap, tp["kap"].ap, tp["ePrev"].ap, ALU.mult, [tp["kap"], tp["ePrev"]], [o["kapT"]])
                    P.tt("pool", o["bhT"].ap, tp["beta"].ap, tp["enL"].ap, ALU.mult, [tp["beta"], tp["enL"]], [o["bhT"]])
                    P.tt("dve", o["btT"].ap, tp["beta"].ap, tp["eEnd"].ap, ALU.mult, [tp["beta"], tp["eEnd"]], [o["btT"]])
                    P.tt("pool", o["ktT"].ap, tp["kdir"].ap, tp["eEnd"].ap, ALU.mult, [tp["kdir"], tp["eEnd"]], [o["ktT"]])
                    yield
                    for c in range(4):
                        P.ts("pool", o["Dg"].ap[:, c * 64:(c + 1) * 64], ident8.ap[:, 0:64], gC.ap[:, c:c + 1], None, ALU.mult, None,
                             [ident8, gC], [o["Dg"]])
                pairs = [(0, 1), (2, 3)] if d == 0 else [(3, 2), (1, 0)]
                for cs in pairs:
                    def each():
                        for ci, c in enumerate(cs):
                            for hp in range(4):
                                for par in range(2):
                                    yield ci, c, hp, par * 64, ci * 4 + hp

                    def sl(tb, po, g, w=64, off=0):
                        return tb.ap[po:po + 64, g * w + off:g * w + off + 64]

                    def osl(q, hp, po, c):
                        return ops_[q][hp].ap[po:po + 64, c * 64:(c + 1) * 64]

                    def amat(lq, rq):
                        ps = PSS.next()
                        for ci, c, hp, po, g in each():
                            P.mm(ps.ap[po:po + 64, g * 64:(g + 1) * 64], osl(lq, hp, po, c), osl(rq, hp, po, c), True, True,
                                 [ops_[lq][hp], ops_[rq][hp]], [ps])
                        return ps

                    ps = amat("bhT", "kapT")
                    P.stt(pt["N"].ap, ps.ap, -1.0, mT_s, ALU.mult, ALU.mult, [ps, masks], [pt["N"]])
                    yield
                    ps = amat("kapT", "bhT")
                    P.stt(pt["NT"].ap, ps.ap, -1.0, m_s, ALU.mult, ALU.mult, [ps, masks], [pt["NT"]])
                    yield
                    ps = amat("KhT", "kapT")
                    P.tt("dve", pt["AkkT"].ap, ps.ap, mT_s, ALU.mult, [ps, masks], [pt["AkkT"]])
                    yield
                    ps = amat("KhT", "RhT")
                    P.tt("dve", pt["BrkT"].ap, ps.ap, mT_i, ALU.mult, [ps, masks], [pt["BrkT"]])
                    yield
                    ps = amat("bhT", "RhT")
                    P.tt("dve", W2v[:, :, 0:64], v3(ps), mT_i.rearrange("p (g i) -> p g i", i=64), ALU.mult, [ps, masks], [W2])
                    yield
                    P.tt("pool", pt["Xf"].ap, pt["N"].ap, ident8.ap, ALU.add, [pt["N"], ident8], [pt["Xf"]])
                    P.tt("dve", pt["X"].ap, pt["N"].ap, ident8.ap, ALU.add, [pt["N"], ident8], [pt["X"]])
                    cur = (pt["N"], pt["NT"])
                    nxt = (pt["N2"], pt["NT2"])
                    for m in range(1, 6):
                        if m < 5:
                            ps = PSS.next()
                            for ci, c, hp, po, g in each():
                                P.mm(ps.ap[po:po + 64, g * 64:(g + 1) * 64], sl(cur[1], po, g), sl(cur[0], po, g), True, True,
                                     [cur[0], cur[1]], [ps])
                            P.cp("act", nxt[0].ap, ps.ap, [ps], [nxt[0]])
                            yield
                        ps = PSS.next()
                        for ci, c, hp, po, g in each():
                            P.mm(ps.ap[po:po + 64, g * 64:(g + 1) * 64], sl(cur[0], po, g), sl(cur[1], po, g), True, True,
                                 [cur[0], cur[1]], [ps])
                        P.cp("act", nxt[1].ap, ps.ap, [ps], [nxt[1]])
                        yield
                        ps = PSS.next()
                        for ci, c, hp, po, g in each():
                            P.mm(ps.ap[po:po + 64, g * 64:(g + 1) * 64], sl(nxt[1], po, g), sl(pt["X"], po, g), True, True,
                                 [nxt[1], pt["X"]], [ps])
                        P.tt("dve", pt["X"].ap, ps.ap, pt["Xf"].ap, ALU.add, [ps, pt["Xf"]], [pt["X"]])
                        yield
                        if m < 5:
                            P.tt("dve", pt["Xf"].ap, ps.ap, pt["Xf"].ap, ALU.add, [ps, pt["Xf"]], [pt["Xf"]])
                        cur, nxt = nxt, cur
                    for qi, (srcq, dst3, dtb) in enumerate((("kapT", RHS1v[:, :, 0:64], RHS1), ("vT", v3(pt["Vtok"]), pt["Vtok"]),
                                                            ("btT", W2v[:, :, 64:128], W2), ("ktT", v3(pt["kttok"]), pt["kttok"]))):
                        ps = PSS.next()
                        for ci, c, hp, po, g in each():
                            P.mm(ps.ap[po:po + 64, g * 64:(g + 1) * 64], osl(srcq, hp, po, c), identb.ap[po:po + 64, po:po + 64], True, True,
                                 [ops_[srcq][hp], identb], [ps])
                        P.cp("act" if qi % 2 == 0 else "dve", dst3, v3(ps), [ps], [dtb])
                        yield
                    ps = PSS.next()
                    for ci, c, hp, po, g in each():
                        P.mm(ps.ap[po:po + 64, g * 64:(g + 1) * 64], sl(pt["AkkT"], po, g), sl(pt["Vtok"], po, g), True, True,
                             [pt["AkkT"], pt["Vtok"]], [ps])
                    P.cp("act", RHS1v[:, :, 64:128], v3(ps), [ps], [RHS1])
                    yield
                    for ci in range(2):
                        ps = PSS.next()
                        for hp in range(4):
                            for par in range(2):
                                po = par * 64
                                g = ci * 4 + hp
                                P.mm(ps.ap[po:po + 64, hp * 128:(hp + 1) * 128], sl(pt["X"], po, g),
                                     RHS1.ap[po:po + 64, g * 128:(g + 1) * 128], True, True, [pt["X"], RHS1], [ps])
                        P.ts("dve", negPU.ap[:, ci * 512:(ci + 1) * 512], ps.ap, -1.0, None, ALU.mult, None, [ps], [negPU])
                        yield
                    ps = PSS.next()
                    for ci, c, hp, po, g in each():
                        P.mm(ps.ap[po:po + 64, g * 64:(g + 1) * 64], identb.ap[po:po + 64, po:po + 64], osl("RhT", hp, po, c), True, False,
                             [identb, ops_["RhT"][hp]], [ps])
                        P.mm(ps.ap[po:po + 64, g * 64:(g + 1) * 64], sl(negPU, po, g, 128, 0), sl(W2, po, g, 128, 0), False, True,
                             [negPU, W2], [ps])
                    P.cp("act", pt["QT"].ap, ps.ap, [ps], [pt["QT"]])
                    yield
                    ps = PSS.next()
                    for ci, c, hp, po, g in each():
                        P.mm(ps.ap[po:po + 64, g * 64:(g + 1) * 64], identb.ap[po:po + 64, po:po + 64], osl("Dg", hp, po, c), True, False,
                             [identb, ops_["Dg"][hp]], [ps])
                        P.mm(ps.ap[po:po + 64, g * 64:(g + 1) * 64], sl(negPU, po, g, 128, 0), sl(W2, po, g, 128, 64), False, True,
                             [negPU, W2], [ps])
                    P.cp("dve", pt["PhiT"].ap, ps.ap, [ps], [pt["PhiT"]])
                    yield
                    ps = PSS.next()
                    for ci, c, hp, po, g in each():
                        P.mm(ps.ap[po:po + 64, g * 64:(g + 1) * 64], sl(pt["Vtok"], po, g), sl(pt["BrkT"], po, g), True, False,
                             [pt["Vtok"], pt["BrkT"]], [ps])
                        P.mm(ps.ap[po:po + 64, g * 64:(g + 1) * 64], sl(negPU, po, g, 128, 64), sl(W2, po, g, 128, 0), False, True,
                             [negPU, W2], [ps])
                    P.cp("act", pt["Y0T"].ap, ps.ap, [ps], [pt["Y0T"]])
                    yield
                    ps = PSS.next()
                    for ci, c, hp, po, g in each():
                        P.mm(ps.ap[po:po + 64, g * 64:(g + 1) * 64], sl(pt["kttok"], po, g), sl(pt["Vtok"], po, g), True, False,
                             [pt["kttok"], pt["Vtok"]], [ps])
                        P.mm(ps.ap[po:po + 64, g * 64:(g + 1) * 64], sl(W2, po, g, 128, 64), sl(negPU, po, g, 128, 64), False, True,
                             [W2, negPU], [ps])
                    P.cp("dve", pt["Psi"].ap, ps.ap, [ps], [pt["Psi"]])
                    yield
                    for ci, c in enumerate(cs):
                        Hc, Hn = Hs[hi % 2], Hs[(hi + 1) % 2]
                        yo = youts[hi % 2]
                        hi += 1
                        ps = PSS.next()
                        for hp in range(4):
                            for par in range(2):
                                po = par * 64
                                g = ci * 4 + hp
                                P.mm(ps.ap[po:po + 64, hp * 64:(hp + 1) * 64], sl(Hc, po, hp), sl(pt["QT"], po, g), True, True,
                                     [Hc, pt["QT"]], [ps])
                                P.mm(ps.ap[po:po + 64, 256 + hp * 64:256 + (hp + 1) * 64], sl(pt["PhiT"], po, g), sl(Hc, po, hp), True, True,
                                     [Hc, pt["PhiT"]], [ps])
                        P.tt("dve", yo.ap, ps.ap[:, 0:256], pt["Y0T"].ap[:, ci * 256:(ci + 1) * 256], ALU.add, [ps, pt["Y0T"]], [yo])
                        P.tt("dve", Hn.ap, ps.ap[:, 256:512], pt["Psi"].ap[:, ci * 256:(ci + 1) * 256], ALU.add, [ps, pt["Psi"]], [Hn])
                        tg = t0 + c * 64
                        P.dma("sp", yS[d][:, :, tg:tg + 64], v3(yo), [yo], [yS_b[d][u * 4 + c]])
                        yield
        def precast_gen(l=l):
            wl = WL(4)
            jobs = []
            for fc in range(8):
                jobs.append((wview(wf_d[l], fc * 128, 0, 4), 4, wfB[fc], wfB_b[fc]))
                jobs.append((wview(wr_d[l], fc * 128, 0, 4), 4, wrB[fc], wrB_b[fc]))
                jobs.append((wview(wo_d[l], fc * 128), 8, woB[fc], woB_b[fc]))
            for fc in range(32):
                jobs.append((wview(w1_d[l], fc * 128), 8, w1B[fc], w1B_b[fc]))
            for fc in range(8):
                for q in range(4):
                    jobs.append((wview(w2_d[l], fc * 128, q * 8, 8), 8, w2B[fc * 4 + q], w2B_b[fc * 4 + q]))
            for (src3, kc, dst, dstb), wb in zip(jobs, prefetched(wl, [(j[0], j[1]) for j in jobs])):
                P.dma("sp", dst[:, 0:kc * 128], wb.ap[:, 0:kc * 128], [wb], [dstb])
                yield
                yield
                yield

        gens = [scan_dir(0), scan_dir(1), precast_gen()]
        while gens:
            for g_ in list(gens):
                try:
                    next(g_)
                except StopIteration:
                    gens.remove(g_)
        if stop_after == "p2":
            break

        for ti, (t0, nt) in enumerate(TILES):
            if last and ti == 0:
                continue
            col = 1 if ti == 0 else 0
            P.barrier()
            A.reset(arena_base)
            xt = A.f32(4096, "xt")
            x3 = xt.ap[:, 0:8 * nt].rearrange("p (k t) -> p k t", k=8)
            P.dma("sp", x3, xsrc[:, :, t0:t0 + nt], [xS_b[ti]] if l > 0 else [], [xt])
            rwk = A.bf16(4 * 512, "rwk")
            rwk3 = rwk.ap[:, 0:4 * nt].rearrange("p (g t) -> p g t", g=4)
            mark2 = A.off
            y0 = A.f32(2048, "y0")
            y1 = A.f32(2048, "y1")
            y03 = y0.ap[:, 0:4 * nt].rearrange("p (g t) -> p g t", g=4)
            y13 = y1.ap[:, 0:4 * nt].rearrange("p (g t) -> p g t", g=4)
            cl = [u_ * 4 + c_ for u_ in range(t0 // 256, (t0 + nt) // 256) for c_ in range(4)]
            P.dma("sp", y03, yS[0][:, :, t0:t0 + nt], [yS_b[0][c_] for c_ in cl], [y0])
            P.dma("sp", y13, yS[1][:, :, t0:t0 + nt], [yS_b[1][c_] for c_ in cl], [y1])
            P.tt("pool", y0.ap[:, 0:4 * nt], y0.ap[:, 0:4 * nt], y1.ap[:, 0:4 * nt], ALU.add, [y0, y1], [y0])
            adt = A.f32(512, "adt")
            sgt = A.f32(512, "sgt")
            P.dma("sp", adt.ap[:, 0:nt], rwS[13][:, t0:t0 + nt], [rwS_b[13]], [adt])
            P.dma("sp", sgt.ap[:, 0:nt], rwS[14][:, t0:t0 + nt], [rwS_b[14]], [sgt])
            P.actf(sgt.ap[:, 0:nt], sgt.ap[:, 0:nt], AF.Sigmoid, [sgt], [sgt])
            cn = ("r", "k", "v", "yc", "sq", "sd", "yn", "a0", "a1", "tq", "ks", "prod", "bon")
            ct_ = [{n: A.f32(512, n) for n in cn} for _ in range(2)]
            def p2c_hp(hp):
                tq = ct_[hp % 2]
                w = {n: tq[n].ap[:, 0:nt] for n in cn}
                P.dma("sp", w["r"], rwS[hp][:, t0:t0 + nt], [rwS_b[hp]], [tq["r"]])
                P.dma("sp", w["k"], rwS[4 + hp][:, t0:t0 + nt], [rwS_b[4 + hp]], [tq["k"]])
                P.dma("sp", w["v"], rwS[8 + hp][:, t0:t0 + nt], [rwS_b[8 + hp]], [tq["v"]])
                pA = PS.next()
                P.mm(pA.ap[:, 0:nt], onesblkm.ap, y03[:, hp, :], True, True, [onesblkm, y0], [pA])
                P.tt("dve", w["yc"], y03[:, hp, :], pA.ap[:, 0:nt], ALU.subtract, [y0, pA], [tq["yc"]])
                yield
                P.actf(w["sq"], w["yc"], AF.Square, [tq["yc"]], [tq["sq"]])
                pB = PS.next()
                P.mm(pB.ap[:, 0:nt], onesblkm.ap, w["sq"], True, True, [onesblkm, tq["sq"]], [pB])
                P.actf(w["sd"], pB.ap[:, 0:nt], AF.Sqrt, [pB, eps_g], [tq["sd"]], bias=eps_g.ap)
                yield
                P.recip(w["sd"], w["sd"], [tq["sd"]], [tq["sd"]])
                P.tt("dve", w["yn"], w["yc"], w["sd"], ALU.mult, [tq["yc"], tq["sd"]], [tq["yn"]])
                P.actf(w["yn"], w["yn"], AF.Identity, [tq["yn"], pv[l]], [tq["yn"]], bias=pv[l].ap[:, 32 + hp:33 + hp],
                       scale=pv[l].ap[:, 28 + hp:29 + hp])
                pC = PS.next()
                P.mm(pC.ap[:, 0:nt], aup[l][0].ap[:, hp * 128:(hp + 1) * 128], adt.ap[:, 0:nt], True, True, [aup[l][0], adt], [pC])
                P.actf(w["a0"], pC.ap[:, 0:nt], AF.Sigmoid, [pC, pv[l]], [tq["a0"]], bias=pv[l].ap[:, 8 + hp:9 + hp])
                yield
                pD = PS.next()
                P.mm(pD.ap[:, 0:nt], aup[l][1].ap[:, hp * 128:(hp + 1) * 128], adt.ap[:, 0:nt], True, True, [aup[l][1], adt], [pD])
                P.actf(w["a1"], pD.ap[:, 0:nt], AF.Sigmoid, [pD, pv[l]], [tq["a1"]], bias=pv[l].ap[:, 12 + hp:13 + hp])
                yield
                P.tt("pool", w["a0"], w["a0"], w["a1"], ALU.add, [tq["a0"], tq["a1"]], [tq["a0"]])
                P.ts("dve", w["tq"], w["a0"], pv[l].ap[:, 20 + hp:21 + hp], c2[l].ap[:, hp:hp + 1], ALU.mult, ALU.add,
                     [tq["a0"], pv[l], c2[l]], [tq["tq"]])
                P.tt("pool", w["ks"], w["k"], w["tq"], ALU.mult, [tq["k"], tq["tq"]], [tq["ks"]])
                P.stt(w["prod"], w["r"], pv[l].ap[:, 24 + hp:25 + hp], w["ks"], ALU.mult, ALU.mult, [tq["r"], pv[l], tq["ks"]], [tq["prod"]])
                pE = PS.next()
                P.mm(pE.ap[:, 0:nt], onesblk.ap, w["prod"], True, True, [onesblk, tq["prod"]], [pE])
                P.tt("dve", w["bon"], w["v"], pE.ap[:, 0:nt], ALU.mult, [tq["v"], pE], [tq["bon"]])
                yield
                P.tt("pool", w["bon"], w["bon"], w["yn"], ALU.add, [tq["bon"], tq["yn"]], [tq["bon"]])
                pG = PS.next()
                P.mm(pG.ap[:, 0:nt], gup[l].ap[:, hp * 128:(hp + 1) * 128], sgt.ap[:, 0:nt], True, True, [gup[l], sgt], [pG])
                P.tt("dve", rwk3[:, hp, :], w["bon"], pG.ap[:, 0:nt], ALU.mult, [tq["bon"], pG], [rwk])
                yield
            for hpp in ((0, 1), (2, 3)):
                gens = [p2c_hp(hpp[0]), p2c_hp(hpp[1])]
                while gens:
                    for g_ in list(gens):
                        try:
                            next(g_)
                        except StopIteration:
                            gens.remove(g_)
            if dbg and l == 0:
                P.dma("sp", dbg_rwk[:, :, t0:t0 + nt], rwk3, [rwk], [])
            P.barrier()
            A.reset(mark2)
            wl = BL(5)
            fo = A.bf16(4 * 512, "fo")
            fo3 = fo.ap[:, 0:4 * nt].rearrange("p (g t) -> p g t", g=4)
            P.dma("sp", fo3, foutS.rearrange("g p t -> p g t")[:, :, t0:t0 + nt], [foutS_b[g][ti] for g in range(4)], [fo])
            gt = A.bf16(16 * 512, "gt")
            gt3 = gt.ap[:, 0:16 * nt].rearrange("p (g t) -> p g t", g=16)
            P.dma("sp", gt3, gateS.rearrange("g p t -> p g t")[:, :, t0:t0 + nt], gateS_b, [gt])
            mg = A.bf16(8 * 512, "mg")
            mg3 = mg.ap[:, 0:8 * nt].rearrange("p (g t) -> p g t", g=8)
            m1s = [A.f32(512, "m1") for _ in range(2)]
            m2s = [A.f32(512, "m2") for _ in range(2)]
            specs = []
            for fc in range(8):
                specs.append((wfB[fc], 512, wfB_b[fc]))
                specs.append((wrB[fc], 512, wrB_b[fc]))
            it = prefetched(wl, specs)
            for fc in range(8):
                wfb = next(it)
                wrb = next(it)
                wf3 = wfb.ap[:, 0:512].rearrange("p (k n) -> p k n", k=4)
                wr3 = wrb.ap[:, 0:512].rearrange("p (k n) -> p k n", k=4)
                pF, pR = PS.next(), PS.next()
                for kc in range(4):
                    P.mm(pF.ap[:, 0:nt], wf3[:, kc, :], fo3[:, kc, :], kc == 0, kc == 3, [wfb, fo], [pF])
                for kc in range(4):
                    P.mm(pR.ap[:, 0:nt], wr3[:, kc, :], rwk3[:, kc, :], kc == 0, kc == 3, [wrb, rwk], [pR])
                m1, m2 = m1s[fc % 2], m2s[fc % 2]
                P.tt("dve", m1.ap[:, 0:nt], pF.ap[:, 0:nt], gt3[:, fc, :], ALU.mult, [pF, gt], [m1])
                P.tt("dve", m2.ap[:, 0:nt], pR.ap[:, 0:nt], gt3[:, 8 + fc, :], ALU.mult, [pR, gt], [m2])
                P.tt("pool", mg3[:, fc, :], m1.ap[:, 0:nt], m2.ap[:, 0:nt], ALU.add, [m1, m2], [mg])
            specs = [(woB[fc], 1024, woB_b[fc]) for fc in range(8)]
            for fc, wob in enumerate(prefetched(wl, specs)):
                wo3 = wob.ap.rearrange("p (k n) -> p k n", k=8)
                ps = PS.next()
                for kc in range(8):
                    P.mm(ps.ap[:, 0:nt], wo3[:, kc, :], mg3[:, kc, :], kc == 0, kc == 7, [wob, mg], [ps])
                P.stt(x3[:, fc, :], ps.ap[:, 0:nt], modcol(l, 2, fc, col), x3[:, fc, :], ALU.mult, ALU.add, [ps, mod[l], xt], [xt])
            if dbg and l == 0:
                P.dma("sp", dbg_mg[:, :, t0:t0 + nt], mg3, [mg], [])
                P.dma("sp", dbg_xmid[:, :, t0:t0 + nt], x3, [xt], [])
            P.barrier()
            A.reset(mark2)
            wl = BL(5)
            sq = A.f32(4096, "sq")
            rstd = A.f32(512, "rstd")
            tmp = [A.f32(512, "tmp") for _ in range(2)]
            h2 = A.bf16(8 * 512, "h2")
            h23 = h2.ap[:, 0:8 * nt].rearrange("p (k t) -> p k t", k=8)
            norm_mod(l, 1, 3, col, x3, nt, sq, rstd, tmp, xt, h23, h2)
            uu = A.bf16(32 * 512, "u")
            u3 = uu.ap[:, 0:32 * nt].rearrange("p (k t) -> p k t", k=32)
            rl = [A.f32(512, "relu") for _ in range(2)]
            specs = [(w1B[fc], 1024, w1B_b[fc]) for fc in range(32)]
            for fc, w1b in enumerate(prefetched(wl, specs)):
                w13 = w1b.ap.rearrange("p (k n) -> p k n", k=8)
                ps = PS.next()
                for kc in range(8):
                    P.mm(ps.ap[:, 0:nt], w13[:, kc, :], h23[:, kc, :], kc == 0, kc == 7, [w1b, h2], [ps])
                r_ = rl[fc % 2]
                P.actf(r_.ap[:, 0:nt], ps.ap[:, 0:nt], AF.Relu, [ps], [r_])
                P.tt("pool", u3[:, fc, :], r_.ap[:, 0:nt], r_.ap[:, 0:nt], ALU.mult, [r_], [uu])
            specs = [(w2B[fc * 4 + q], 1024, w2B_b[fc * 4 + q]) for fc in range(8) for q in range(4)]
            it = prefetched(wl, specs)
            for fc in range(8):
                ps = PS.next()
                for q in range(4):
                    w2b = next(it)
                    w23 = w2b.ap.rearrange("p (k n) -> p k n", k=8)
                    for kk in range(8):
                        P.mm(ps.ap[:, 0:nt], w23[:, kk, :], u3[:, q * 8 + kk, :], q == 0 and kk == 0, q == 3 and kk == 7, [w2b, uu], [ps])
                P.stt(x3[:, fc, :], ps.ap[:, 0:nt], modcol(l, 5, fc, col), x3[:, fc, :], ALU.mult, ALU.add, [ps, mod[l], xt], [xt])
            if not last:
                P.dma("sp", xS[:, :, t0:t0 + nt], x3, [xt], [xS_b[ti]])
            else:
                rms_rstd(x3, nt, sq, rstd, xt)
                sq3 = sq.ap[:, 0:8 * nt].rearrange("p (k t) -> p k t", k=8)
                for k in range(8):
                    P.stt(sq3[:, k, :], x3[:, k, :], norms.ap[:, 32 + k:33 + k], rstd.ap[:, 0:nt], ALU.mult, ALU.mult,
                          [xt, norms, rstd], [sq])
                P.dma("sp", out_d[:, :, t0 - 256:t0 - 256 + nt], sq3, [sq], [])
    P.emit()
    return nc


def _chunk128(v):
    return np.ascontiguousarray(v.reshape(-1, 128).T)


def _consts():
    idx = np.arange(64)
    mU = (idx[:, None] < idx[None, :]).astype(np.float32)
    mL = (idx[:, None] > idx[None, :]).astype(np.float32)
    eye = np.eye(64, dtype=np.float32)

    def tile8(m):
        return np.tile(np.tile(m, (2, 1))[:, None, :], (1, 8, 1)).reshape(128, 512)

    masks = np.concatenate([tile8(mU), tile8(mL), tile8(mU + eye), tile8(mL + eye)], axis=1)
    ident8 = tile8(eye)
    onesblk = np.zeros((128, 128), np.float32)
    onesblk[:64, :64] = 1.0
    onesblk[64:, 64:] = 1.0
    n = np.arange(128)
    ang = 2.0 * np.pi * np.outer(n, n) / 128.0
    dftc = np.concatenate([np.cos(ang), np.sin(ang)], axis=1) / np.sqrt(128.0)

    def seq_tabs(L):
        t = np.arange(L)
        a = 2.0 * np.pi * ((np.outer(t, t)) % L) / L
        return (np.cos(a) / np.sqrt(L)).astype(ml_dtypes.bfloat16), (-np.sin(a) / np.sqrt(L)).astype(ml_dtypes.bfloat16)

    cosL, sinL = seq_tabs(2048)
    cosC, sinC = seq_tabs(256)
    return {
        "ident": np.eye(128, dtype=np.float32), "onesblk": onesblk, "masks": np.ascontiguousarray(masks),
        "ident8": ident8, "dftc": dftc.astype(ml_dtypes.bfloat16), "cosL": cosL, "sinL": sinL, "cosC": cosC, "sinC": sinC,
    }


def make_in_maps(x, c, ctx, c_ctx, w_mod, b_mod, norm1, norm2, w_in, mu_shift, w0, w_up, a0, a_up, g_up, k_k, k_a, r_k,
                 ln_x_w, ln_x_b, w_fourier_up, w_rwkv_up, w_out, mlp_w1, mlp_w2, norm_f):
    f = lambda a: np.ascontiguousarray(np.asarray(a, dtype=np.float32))
    x, c, ctx, c_ctx = f(x), f(c), f(ctx), f(c_ctx)
    shared = dict(_consts())
    shared["w_mod"] = f(w_mod)
    shared["bmod"] = np.stack([_chunk128(f(b_mod)[l]) for l in range(NL)])
    shared["norms"] = np.concatenate([_chunk128(f(norm1)[0]), _chunk128(f(norm1)[1]), _chunk128(f(norm2)[0]),
                                      _chunk128(f(norm2)[1]), _chunk128(f(norm_f))], axis=1)
    shared["w_in"] = f(w_in)
    shared["mu"] = np.stack([_chunk128(f(mu_shift)[l][:1920]) for l in range(NL)])
    pv = []
    for l in range(NL):
        cols = [_chunk128(f(w0)[l, 0]), _chunk128(f(w0)[l, 1]), _chunk128(f(a0)[l, 0]), _chunk128(f(a0)[l, 1]),
                _chunk128(f(k_k)[l]), _chunk128(f(k_a)[l]), _chunk128(f(r_k)[l].reshape(-1)), _chunk128(f(ln_x_w)[l]),
                _chunk128(f(ln_x_b)[l])]
        pv.append(np.concatenate(cols, axis=1))
    shared["pv"] = np.stack(pv)
    wup = np.zeros((NL, 2, 128, 512), np.float32)
    aup = np.zeros((NL, 2, 128, 512), np.float32)
    for l in range(NL):
        for d in range(2):
            wup[l, d, d * 64:(d + 1) * 64] = f(w_up)[l, d]
            aup[l, d, d * 64:(d + 1) * 64] = f(a_up)[l, d]
    shared["wup"] = wup
    shared["aup"] = aup
    shared["gup"] = f(g_up)
    shared["w_f"] = f(w_fourier_up)
    shared["w_r"] = f(w_rwkv_up)
    shared["w_o"] = f(w_out)
    shared["w1"] = f(mlp_w1)
    shared["w2"] = f(mlp_w2)
    in_maps = []
    for b in range(x.shape[0]):
        X = np.concatenate([ctx[b], x[b]], axis=0)
        xT = np.ascontiguousarray(X.T.reshape(8, 128, T).transpose(1, 0, 2))
        cv = np.stack([_chunk128(c[b]), _chunk128(c_ctx)], axis=2).reshape(128, 16)
        m = dict(shared)
        m["xT"] = xT
        m["cvec"] = np.ascontiguousarray(cv)
        in_maps.append(m)
    return in_maps


_NC_CACHE = {}


def kernel(**inputs):
    in_maps = make_in_maps(**inputs)
    if "nc" not in _NC_CACHE:
        _NC_CACHE["nc"] = build_program()
    nc = _NC_CACHE["nc"]
    res = run_bass_kernel_spmd(nc, in_maps, core_ids=list(range(len(in_maps))))
    outs = []
    for r in res.results:
        oT = np.asarray(r["outT"])
        outs.append(oT.transpose(1, 0, 2).reshape(1024, TLAT).T)
    return np.ascontiguousarray(np.stack(outs).astype(np.float32))
```

```python
import numpy as np
import ml_dtypes
import concourse.bass as bass
import concourse.mybir as mybir
from concourse.bass_utils import run_bass_kernel_spmd
from contextlib import ExitStack

F32 = mybir.dt.float32
BF16 = mybir.dt.bfloat16
AF = mybir.ActivationFunctionType
ALU = mybir.AluOpType

ENGS = ("pe", "act", "dve", "pool", "sp")
DMA_SLOTS = 8
EPOCH = 30000

NL = 2
T = 2304
TCTX = 256
TLAT = 2048
TILES = [(0, 256), (256, 512), (768, 512), (1280, 512), (1792, 512)]
NEG_E = -0.6065306597126334


class Buf:
    __slots__ = ("name", "writers", "readers")

    def __init__(self, name=""):
        self.name = name
        self.writers = []
        self.readers = []


class TB:
    def __init__(self, ap, bufs):
        self.ap = ap
        self.bufs = bufs


class Op:
    __slots__ = ("eng", "fn", "dma", "deps", "signal", "ticket", "idx")

    def __init__(self, eng, fn, dma):
        self.eng = eng
        self.fn = fn
        self.dma = dma
        self.deps = []
        self.signal = dma
        self.ticket = None
        self.idx = None


def _bufs(lst):
    out = []
    for t in lst:
        if isinstance(t, Buf):
            out.append(t)
        else:
            out.extend(t.bufs)
    return out


class Prog:
    def __init__(self, nc):
        self.nc = nc
        self.ops = []
        self.stack = ExitStack()
        self.per_eng_ops = {e: [] for e in ENGS}
        self.pending_bar = {e: set() for e in ENGS}
        self.dmas_since_bar = []

    def sbuf(self, name, shape, dtype=F32):
        return self.stack.enter_context(self.nc.sbuf_tensor(name, list(shape), dtype))

    def psum(self, name, shape, dtype=F32):
        return self.stack.enter_context(self.nc.psum_tensor(name, list(shape), dtype))

    def _add(self, eng, fn, R, W, dma=False):
        reads = _bufs(R)
        writes = _bufs(W)
        op = Op(eng, fn, dma)
        op.idx = len(self.ops)
        deps = set()
        for b in reads:
            deps.update(b.writers)
        for b in writes:
            deps.update(b.writers)
            deps.update(b.readers)
        if self.pending_bar[eng]:
            deps.update(self.pending_bar[eng])
            self.pending_bar[eng] = set()
        op.deps = deps
        if dma:
            self.dmas_since_bar.append(op.idx)
        for b in reads:
            b.readers.append(op.idx)
        for b in writes:
            b.writers = [op.idx]
            b.readers = []
        self.ops.append(op)
        self.per_eng_ops[eng].append(op)
        return op

    def barrier(self):
        deps = set(self.dmas_since_bar)
        for e in ENGS:
            comp = [op.idx for op in self.per_eng_ops[e] if not op.dma]
            if comp:
                deps.add(comp[-1])
        for e in ENGS:
            self.pending_bar[e] = set(deps) | self.pending_bar[e]
        self.dmas_since_bar = []

    def mm(self, out, lhsT, rhs, start, stop, R, W):
        self._add("pe", lambda e: e.matmul(out, lhsT=lhsT, rhs=rhs, start=start, stop=stop), R, W)

    def actf(self, out, in_, func, R, W, bias=None, scale=None):
        kw = {}
        if bias is not None:
            kw["bias"] = bias
        if scale is not None:
            kw["scale"] = scale
        self._add("act", lambda e: e.activation(out=out, in_=in_, func=func, **kw), R, W)

    def tt(self, eng, out, in0, in1, op, R, W):
        self._add(eng, lambda e: e.tensor_tensor(out=out, in0=in0, in1=in1, op=op), R, W)

    def ts(self, eng, out, in0, s1, s2, op0, op1, R, W):
        if s2 is None:
            self._add(eng, lambda e: e.tensor_scalar(out=out, in0=in0, scalar1=s1, scalar2=None, op0=op0), R, W)
        else:
            self._add(eng, lambda e: e.tensor_scalar(out=out, in0=in0, scalar1=s1, scalar2=s2, op0=op0, op1=op1), R, W)

    def stt(self, out, in0, scalar, in1, op0, op1, R, W):
        self._add("dve", lambda e: e.scalar_tensor_tensor(out=out, in0=in0, scalar=scalar, in1=in1, op0=op0, op1=op1), R, W)

    def cp(self, eng, out, in_, R, W):
        if eng == "act":
            self._add("act", lambda e: e.copy(out=out, in_=in_), R, W)
        else:
            self._add(eng, lambda e: e.tensor_copy(out=out, in_=in_), R, W)

    def memset(self, eng, ap, val, W):
        self._add(eng, lambda e: e.memset(ap, val), [], W)

    def scan(self, out, d0, d1, R, W):
        self._add("dve", lambda e: e.tensor_tensor_scan(out=out, data0=d0, data1=d1, initial=0.0, op0=ALU.mult, op1=ALU.add), R, W)

    def recip(self, out, in_, R, W):
        self._add("dve", lambda e: e.reciprocal(out=out, in_=in_), R, W)

    def dma(self, q, out, in_, R, W):
        self._add(q, lambda e: e.dma_start(out=out, in_=in_), R, W, dma=True)

    def emit(self):
        nc = self.nc
        ops = self.ops
        pos = {}
        for e in ENGS:
            for i, op in enumerate(self.per_eng_ops[e]):
                pos[op.idx] = i
        for op in ops:
            best = {}
            dmas = []
            for d in op.deps:
                p = ops[d]
                if p.dma:
                    dmas.append(d)
                    continue
                if p.eng == op.eng and not op.dma:
                    if p.eng == "pe":
                        continue
                if p.eng not in best or pos[d] > pos[best[p.eng]]:
                    best[p.eng] = d
            op.deps = sorted(dmas) + sorted(best.values())
            for d in op.deps:
                ops[d].signal = True
        sems = {}

        def getsem(key):
            if key not in sems:
                sems[key] = self.stack.enter_context(nc.semaphore("s_%s_%s" % key))
            return sems[key]

        cnt = {e: 0 for e in ENGS}
        dcnt = {e: 0 for e in ENGS}
        for e in ENGS:
            for op in self.per_eng_ops[e]:
                if op.dma:
                    j = dcnt[e]
                    dcnt[e] += 1
                    op.ticket = (getsem((e + "q", j % DMA_SLOTS)), 16 * (j // DMA_SLOTS + 1))
                elif op.signal:
                    j = cnt[e]
                    cnt[e] += 1
                    op.ticket = (getsem((e, j // EPOCH)), (j % EPOCH) + 1)
        for e in ENGS:
            dl = [op for op in self.per_eng_ops[e] if op.dma]
            for j, op in enumerate(dl):
                if j >= DMA_SLOTS:
                    op.deps = [dl[j - DMA_SLOTS].idx] + [d for d in op.deps if d != dl[j - DMA_SLOTS].idx]
        with nc.Block() as block:
            def make(ename):
                def body(engobj):
                    seen = {}
                    for op in self.per_eng_ops[ename]:
                        need = {}
                        for d in op.deps:
                            sem, val = ops[d].ticket
                            k = id(sem)
                            if seen.get(k, 0) >= val:
                                continue
                            if k not in need or need[k][1] < val:
                                need[k] = (sem, val)
                        for k, (sem, val) in need.items():
                            engobj.wait_ge(sem, val)
                            seen[k] = val
                        ins = op.fn(engobj)
                        if op.ticket is not None:
                            ins.then_inc(op.ticket[0], 16 if op.dma else 1)
                    last = {}
                    for op in self.per_eng_ops[ename]:
                        if op.dma:
                            last[id(op.ticket[0])] = op.ticket
                    for sem, val in last.values():
                        if seen.get(id(sem), 0) < val:
                            engobj.wait_ge(sem, val)
                return body

            if self.per_eng_ops["sp"]:
                block.sync(make("sp"))
            if self.per_eng_ops["act"]:
                block.scalar(make("act"))
            if self.per_eng_ops["dve"]:
                block.vector(make("dve"))
            if self.per_eng_ops["pool"]:
                block.gpsimd(make("pool"))
            if self.per_eng_ops["pe"]:
                block.tensor(make("pe"))
        self.stack.close()


class Arena:
    def __init__(self, P, words):
        self.t = P.sbuf("arena", [128, words], F32)
        self.words = words
        self.off = 0
        self.n = 0

    def reset(self, to=0):
        self.off = to

    def f32(self, n, name="t"):
        ap = self.t[:, self.off:self.off + n]
        self.off += n
        assert self.off <= self.words, "arena overflow %d > %d" % (self.off, self.words)
        self.n += 1
        return TB(ap, [Buf("%s%d" % (name, self.n))])

    def bf16(self, n, name="t"):
        w = (n + 1) // 2
        ap = self.t[:, self.off:self.off + w].bitcast(BF16)
        self.off += w
        assert self.off <= self.words, "arena overflow %d > %d" % (self.off, self.words)
        self.n += 1
        return TB(ap, [Buf("%s%d" % (name, self.n))])


class Rot:
    def __init__(self, items):
        self.items = items
        self.i = 0

    def next(self):
        it = self.items[self.i % len(self.items)]
        self.i += 1
        return it


def build_program(nl=NL, dbg=False, stop_after=None):
    nc = bass.Bass("TRN2", target_bir_lowering=False)
    P = Prog(nc)

    def din(name, shape, dt=F32):
        return nc.dram_tensor(name, list(shape), dt, kind="ExternalInput").ap()

    def dscr(name, shape, dt=F32):
        return nc.dram_tensor(name, list(shape), dt, kind=("ExternalOutput" if dbg else "Internal")).ap()

    xT_d = din("xT", [128, 8, T])
    cvec_d = din("cvec", [128, 16])
    wmod_d = din("w_mod", [NL, 1024, 6144])
    bmod_d = din("bmod", [NL, 128, 48])
    norms_d = din("norms", [128, 40])
    win_d = din("w_in", [NL, 1024, 4480])
    mu_d = din("mu", [NL, 128, 15])
    pv_d = din("pv", [NL, 128, 36])
    wup_d = din("wup", [NL, 2, 128, 512])
    aup_d = din("aup", [NL, 2, 128, 512])
    gup_d = din("gup", [NL, 128, 512])
    wf_d = din("w_f", [NL, 512, 1024])
    wr_d = din("w_r", [NL, 512, 1024])
    wo_d = din("w_o", [NL, 1024, 1024])
    w1_d = din("w1", [NL, 1024, 4096])
    w2_d = din("w2", [NL, 4096, 1024])
    ident_d = din("ident", [128, 128])
    onesblk_d = din("onesblk", [128, 128])
    masks_d = din("masks", [128, 4 * 512])
    ident8_d = din("ident8", [128, 512])
    dftc_d = din("dftc", [128, 256], BF16)
    cosL_d = din("cosL", [2048, 2048], BF16)
    sinL_d = din("sinL", [2048, 2048], BF16)
    cosC_d = din("cosC", [256, 256], BF16)
    sinC_d = din("sinC", [256, 256], BF16)
    out_d = nc.dram_tensor("outT", [128, 8, TLAT], F32, kind="ExternalOutput").ap()

    xS = dscr("xS", [128, 8, T])
    rwS = dscr("rwS", [15, 128, T])
    finS = dscr("finS", [4, 128, T], BF16)
    gateS = dscr("gateS", [16, 128, T], BF16)
    yS = dscr("yS", [2, 128, 4, T])
    foutS = dscr("foutS", [4, 128, T], BF16)
    if dbg:
        dbg_rwk = dscr("dbg_rwk", [128, 4, T], BF16)
        dbg_mg = dscr("dbg_mg", [128, 8, T], BF16)
        dbg_xmid = dscr("dbg_xmid", [128, 8, T])
    wfB = nc.dram_tensor("wfB", [8, 128, 512], BF16, kind="Internal").ap()
    wrB = nc.dram_tensor("wrB", [8, 128, 512], BF16, kind="Internal").ap()
    woB = nc.dram_tensor("woB", [8, 128, 1024], BF16, kind="Internal").ap()
    w1B = nc.dram_tensor("w1B", [32, 128, 1024], BF16, kind="Internal").ap()
    w2B = nc.dram_tensor("w2B", [32, 128, 1024], BF16, kind="Internal").ap()
    wfB_b = [Buf("wfB%d" % i) for i in range(8)]
    wrB_b = [Buf("wrB%d" % i) for i in range(8)]
    woB_b = [Buf("woB%d" % i) for i in range(8)]
    w1B_b = [Buf("w1B%d" % i) for i in range(32)]
    w2B_b = [Buf("w2B%d" % i) for i in range(32)]
    xS_b = [Buf("xS%d" % i) for i in range(5)]
    rwS_b = [Buf("rwS%d" % i) for i in range(15)]
    finS_b = [Buf("finS%d" % i) for i in range(4)]
    gateS_b = [Buf("gateS%d" % i) for i in range(16)]
    yS_b = [[Buf("yS%d_%d" % (d, c)) for c in range(36)] for d in range(2)]
    foutS_b = [[Buf("foutS%d_%d" % (g, i)) for i in range(5)] for g in range(4)]

    def ptile(name, shape, dt=F32):
        t = P.sbuf(name + "_sb", shape, dt)
        return TB(t[:], [Buf(name)])

    ident = ptile("ident", [128, 128])
    onesblk = ptile("onesblk", [128, 128])
    onesblkm = ptile("onesblkm", [128, 128])
    ones128 = ptile("ones128", [128, 128])
    masks = ptile("masks", [128, 2048])
    ident8 = ptile("ident8", [128, 512])
    dftc = ptile("dftc", [128, 256], BF16)
    eps_n = ptile("eps_n", [128, 1])
    eps_g = ptile("eps_g", [128, 1])
    sc = ptile("sc", [128, 16])
    norms = ptile("norms", [128, 40])
    P.dma("sp", ident.ap, ident_d, [], [ident])
    P.dma("sp", onesblk.ap, onesblk_d, [], [onesblk])
    P.dma("sp", masks.ap, masks_d, [], [masks])
    P.dma("sp", ident8.ap, ident8_d, [], [ident8])
    P.dma("sp", dftc.ap, dftc_d, [], [dftc])
    P.dma("sp", sc.ap, cvec_d, [], [sc])
    P.dma("sp", norms.ap, norms_d, [], [norms])
    P.memset("pool", ones128.ap, 1.0, [ones128])
    P.memset("pool", eps_n.ap, 1e-6, [eps_n])
    P.memset("pool", eps_g.ap, 64e-5, [eps_g])
    P.ts("pool", onesblkm.ap, onesblk.ap, 1.0 / 64.0, None, ALU.mult, None, [onesblk], [onesblkm])
    P.actf(sc.ap, sc.ap, AF.Silu, [sc], [sc])
    identb = ptile("identb", [128, 128], BF16)
    P.cp("dve", identb.ap, ident.ap, [ident], [identb])

    mod, cA, mu, omm, pv, omka, c2, wup, aup, gup, bm = [], [], [], [], [], [], [], [], [], [], []
    for l in range(nl):
        mod.append(ptile("mod%d" % l, [128, 96]))
        cA.append(ptile("cA%d" % l, [128, 32]))
        mu.append(ptile("mu%d" % l, [128, 15]))
        omm.append(ptile("omm%d" % l, [128, 15]))
        pv.append(ptile("pv%d" % l, [128, 36]))
        omka.append(ptile("omka%d" % l, [128, 4]))
        c2.append(ptile("c2%d" % l, [128, 4]))
        wup.append([ptile("wup%d_%d" % (l, d), [128, 512]) for d in range(2)])
        aup.append([ptile("aup%d_%d" % (l, d), [128, 512]) for d in range(2)])
        gup.append(ptile("gup%d" % l, [128, 512]))
        bm.append(ptile("bm%d" % l, [128, 48]))
        P.dma("sp", mu[l].ap, mu_d[l], [], [mu[l]])
        P.dma("sp", pv[l].ap, pv_d[l], [], [pv[l]])
        P.dma("sp", bm[l].ap, bmod_d[l], [], [bm[l]])
        P.dma("sp", gup[l].ap, gup_d[l], [], [gup[l]])
        for d in range(2):
            P.dma("sp", wup[l][d].ap, wup_d[l, d], [], [wup[l][d]])
            P.dma("sp", aup[l][d].ap, aup_d[l, d], [], [aup[l][d]])
        P.ts("pool", omm[l].ap, mu[l].ap, -1.0, 1.0, ALU.mult, ALU.add, [mu[l]], [omm[l]])
        P.ts("pool", omka[l].ap, pv[l].ap[:, 20:24], -1.0, 1.0, ALU.mult, ALU.add, [pv[l]], [omka[l]])
        P.ts("pool", c2[l].ap, pv[l].ap[:, 20:24], -2.0, 2.0, ALU.mult, ALU.add, [pv[l]], [c2[l]])

    pst = [P.psum("ps%d" % i, [128, 1024]) for i in range(4)]
    banks = []
    for i in range(4):
        for hlf in range(2):
            banks.append(TB(pst[i][:, hlf * 512:(hlf + 1) * 512], [Buf("bank%d" % (2 * i + hlf))]))
    PS = Rot(banks)

    AW = (nc.sbuf_bytes_remaining - 1024) // 4
    A = Arena(P, AW)

    wm_slots = [A.f32(4096, "wm") for _ in range(2)]
    modrow = A.f32(6144, "modrow")
    for l in range(nl):
        psb = PS.next()
        for g in range(12):
            wm = wm_slots[g % 2]
            src = wmod_d[l].rearrange("(k p) n -> p k n", p=128)[:, :, g * 512:(g + 1) * 512]
            P.dma("sp" if g % 2 == 0 else "pool", wm.ap.rearrange("p (k n) -> p k n", k=8), src, [], [wm])
            wm3 = wm.ap.rearrange("p (k n) -> p k n", k=8)
            pr = PS.next()
            for k in range(8):
                P.mm(pr.ap[0:2, 0:512], sc.ap[:, 2 * k:2 * k + 2], wm3[:, k, :], k == 0, k == 7, [wm, sc], [pr])
            P.cp("act", modrow.ap[0:2, g * 512:(g + 1) * 512], pr.ap[0:2, 0:512], [pr], [modrow])
        for jj in range(48):
            P.mm(psb.ap[:, 2 * jj:2 * jj + 2], modrow.ap[0:2, jj * 128:(jj + 1) * 128], ident.ap[0:2, 0:2], True, True,
                 [modrow, ident], [psb])
        P.tt("dve", mod[l].ap.rearrange("p (j c) -> p j c", c=2), psb.ap[:, 0:96].rearrange("p (j c) -> p j c", c=2),
             bm[l].ap.rearrange("p (j o) -> p j o", o=1).broadcast_to([128, 48, 2]), ALU.add, [psb, bm[l]], [mod[l]])
        m3 = mod[l].ap.rearrange("p (j c) -> p j c", c=2)
        cA3 = cA[l].ap.rearrange("p (w k c) -> p w k c", w=2, k=8)
        for which, (sec, n0) in enumerate(((8, l * 8), (32, 16 + l * 8))):
            for col in range(2):
                P.stt(cA3[:, which, :, col], m3[:, sec:sec + 8, col], 1.0, norms.ap[:, n0:n0 + 8], ALU.add, ALU.mult,
                      [mod[l], norms], [cA[l]])

    def modcol(l, sec, k, col):
        j = ((sec * 8 + k) * 2 + col)
        return mod[l].ap[:, j:j + 1]

    def cAcol(l, which, k, col):
        j = (which * 8 + k) * 2 + col
        return cA[l].ap[:, j:j + 1]

    class WL:
        def __init__(self, ns=3):
            self.st = [A.f32(1024, "wst") for _ in range(ns)]
            self.bf = [A.bf16(1024, "wbf") for _ in range(ns)]
            self.i = 0

        def load(self, src3, kc, parts=128):
            st = self.st[self.i % len(self.st)]
            bf = self.bf[self.i % len(self.bf)]
            n = kc * 128
            P.dma("sp", st.ap[0:parts, 0:n].rearrange("p (k n) -> p k n", k=kc), src3, [], [st])
            eng = "act" if self.i % 2 == 0 else "dve"
            P.cp(eng, bf.ap[0:parts, 0:n], st.ap[0:parts, 0:n], [st], [bf])
            self.i += 1
            return bf

    class BL:
        def __init__(self, ns=4):
            self.bf = [A.bf16(1024, "wbl") for _ in range(ns)]
            self.i = 0

        def load(self, src2, n, dep):
            bf = self.bf[self.i % len(self.bf)]
            self.i += 1
            P.dma("sp", bf.ap[:, 0:n], src2, [dep], [bf])
            return bf

    def wview(w_l, c0, k0=0, kc=8):
        return w_l.rearrange("(k p) n -> p k n", p=128)[:, k0:k0 + kc, c0:c0 + 128]

    def prefetched(wl, specs, ahead=2):
        q = []
        it = iter(specs)
        for _ in range(ahead):
            s_ = next(it, None)
            if s_ is not None:
                q.append(wl.load(*s_))
        for _ in range(len(specs)):
            s_ = next(it, None)
            if s_ is not None:
                q.append(wl.load(*s_))
            yield q.pop(0)

    def rms_rstd(x3, nt, sq, rstd, xtb):
        sq3 = sq.ap[:, 0:8 * nt].rearrange("p (k t) -> p k t", k=8)
        P.actf(sq3, x3, AF.Square, [xtb], [sq])
        ps = PS.next()
        for k in range(8):
            P.mm(ps.ap[:, 0:nt], ones128.ap, sq3[:, k, :], k == 0, k == 7, [ones128, sq], [ps])
        P.actf(rstd.ap[:, 0:nt], ps.ap[:, 0:nt], AF.Sqrt, [ps, eps_n], [rstd], bias=eps_n.ap, scale=1.0 / 1024.0)
        P.recip(rstd.ap[:, 0:nt], rstd.ap[:, 0:nt], [rstd], [rstd])

    def norm_mod(l, which, sec_b, col, x3, nt, sq, rstd, tmps, xtb, out3, outtb):
        rms_rstd(x3, nt, sq, rstd, xtb)
        for k in range(8):
            tmp = tmps[k % 2]
            P.stt(tmp.ap[:, 0:nt], x3[:, k, :], cAcol(l, which, k, col), rstd.ap[:, 0:nt], ALU.mult, ALU.mult,
                  [xtb, cA[l], rstd], [tmp])
            P.actf(out3[:, k, :], tmp.ap[:, 0:nt], AF.Identity, [tmp, mod[l]], [outtb], bias=modcol(l, sec_b, k, col))

    arena_base = 0
    A.reset(arena_base)

    for l in range(nl):
        last = (l == NL - 1)
        xsrc = xT_d if l == 0 else xS
        P.barrier()
        A.reset(arena_base)
        hT = A.bf16(8 * T, "hT")
        hT3 = hT.ap.rearrange("p (k t) -> p k t", k=8)
        mark = A.off
        xts = [A.f32(4096, "xt") for _ in range(2)]
        sq = A.f32(4096, "sq")
        rstd = A.f32(512, "rstd")
        tmp = [A.f32(512, "tmp") for _ in range(2)]
        for ti, (t0, nt) in enumerate(TILES):
            xt = xts[ti % 2]
            x3 = xt.ap[:, 0:8 * nt].rearrange("p (k t) -> p k t", k=8)
            P.dma("sp", x3, xsrc[:, :, t0:t0 + nt], [xS_b[ti]] if l > 0 else [], [xt])
            col = 1 if ti == 0 else 0
            norm_mod(l, 0, 0, col, x3, nt, sq, rstd, tmp, xt, hT3[:, :, t0:t0 + nt], hT)
        P.barrier()
        A.reset(mark)
        wl = WL(3)
        finrows = [A.bf16(T, "finrow") for _ in range(2)]
        rawrows = [A.f32(T, "rawrow") for _ in range(2)]
        outrows = [A.f32(T, "outrow") for _ in range(2)]
        grows = [A.bf16(T, "grow") for _ in range(2)]
        specs = [(wview(win_d[l], fc * 128), 8) for fc in range(35)]
        for fc, wb in enumerate(prefetched(wl, specs)):
            wb3 = wb.ap.rearrange("p (k n) -> p k n", k=8)
            if fc < 4:
                row = finrows[fc % 2]
            elif fc < 19:
                row = rawrows[fc % 2]
            else:
                row = grows[fc % 2]
            for ti, (t0, nt) in enumerate(TILES):
                ps = PS.next()
                for k in range(8):
                    P.mm(ps.ap[:, 0:nt], wb3[:, k, :], hT3[:, k, t0:t0 + nt], k == 0, k == 7, [wb, hT], [ps])
                if fc < 19:
                    P.cp("act", row.ap[:, t0:t0 + nt], ps.ap[:, 0:nt], [ps], [row])
                else:
                    P.actf(row.ap[:, t0:t0 + nt], ps.ap[:, 0:nt], AF.Sigmoid, [ps], [row])
            if fc < 4:
                P.dma("pool", finS[fc], row.ap, [row], [finS_b[fc]])
            elif fc >= 19:
                P.dma("pool", gateS[fc - 19], row.ap, [row], [gateS_b[fc - 19]])
            else:
                j = fc - 4
                orow = outrows[fc % 2]
                P.actf(orow.ap, row.ap, AF.Identity, [row, omm[l]], [orow], scale=omm[l].ap[:, j:j + 1])
                p = 0
                while p < 128:
                    ch = 128 * j + p
                    q = ch // 480
                    hq = ch // 960
                    nb = min((q + 1) * 480, (hq + 1) * 960, 128 * (j + 1)) - 128 * j
                    if p == 32:
                        nb = min(nb, 64)
                    elif p == 96:
                        nb = min(nb, 128)
                    muc = mu[l].ap[p:nb, j:j + 1]
                    R3 = row.ap[p:nb, 256:T].rearrange("p (r c) -> p r c", c=64)
                    O3 = orow.ap[p:nb, 256:T].rearrange("p (r c) -> p r c", c=64)
                    if q == 0:
                        o_, i_ = O3[:, :, 1:64], R3[:, :, 0:63]
                    elif q == 1:
                        o_, i_ = O3[:, :, 0:63], R3[:, :, 1:64]
                    elif q == 2:
                        o_, i_ = O3[:, 1:32, :], R3[:, 0:31, :]
                    else:
                        o_, i_ = O3[:, 0:31, :], R3[:, 1:32, :]
                    P.stt(o_, i_, muc, o_, ALU.mult, ALU.add, [row, mu[l], orow], [orow])
                    Rc = row.ap[p:nb, 0:256]
                    Oc = orow.ap[p:nb, 0:256]
                    if hq == 0:
                        o_, i_ = Oc[:, 1:256], Rc[:, 0:255]
                    else:
                        o_, i_ = Oc[:, 0:255], Rc[:, 1:256]
                    P.stt(o_, i_, muc, o_, ALU.mult, ALU.add, [row, mu[l], orow], [orow])
                    p = nb
                P.dma("pool", rwS[j], orow.ap, [orow], [rwS_b[j]])
        if stop_after == "p1":
            break

        P.barrier()
        A.reset(arena_base)
        finT = A.bf16(4 * T, "finT")
        fin3 = finT.ap.rearrange("p (g t) -> p g t", g=4)
        for g in range(4):
            P.dma("sp", fin3[:, g, :], finS[g], [finS_b[g]], [finT])
        Gcs = [A.bf16(1024, "Gcs") for _ in range(18)]
        for tcg in range(18):
            if last and tcg < 2:
                continue
            psa, psb2 = PS.next(), PS.next()
            for g in range(4):
                pp = psa if g < 2 else psb2
                P.mm(pp.ap[:, (g % 2) * 256:(g % 2) * 256 + 256], fin3[:, g, tcg * 128:(tcg + 1) * 128], dftc.ap,
                     True, True, [finT, dftc], [pp])
            P.cp("act", Gcs[tcg].ap[:, 0:512], psa.ap, [psa], [Gcs[tcg]])
            P.cp("dve", Gcs[tcg].ap[:, 512:1024], psb2.ap, [psb2], [Gcs[tcg]])
        ctabs = [(A.bf16(16 * 512, "cosT"), A.bf16(16 * 512, "sinT")) for _ in range(2)]
        ftiles = [A.bf16(512, "ftile") for _ in range(3)]
        fti = 0
        if not last:
            ct, stt_ = ctabs[0]
            c3 = ct.ap[:, 0:512].rearrange("p (k s) -> p k s", k=2)
            s3 = stt_.ap[:, 0:512].rearrange("p (k s) -> p k s", k=2)
            P.dma("sp", c3, cosC_d.rearrange("(k p) s -> p k s", p=128), [], [ct])
            P.dma("sp", s3, sinC_d.rearrange("(k p) s -> p k s", p=128), [], [stt_])
            for g in range(4):
                ps = PS.next()
                for tc in range(2):
                    P.mm(ps.ap[:, 0:256], Gcs[tc].ap[:, g * 256:g * 256 + 128], c3[:, tc, :], tc == 0, False,
                         [Gcs[tc], ct], [ps])
                    P.mm(ps.ap[:, 0:256], Gcs[tc].ap[:, g * 256 + 128:g * 256 + 256], s3[:, tc, :], False, tc == 1,
                         [Gcs[tc], stt_], [ps])
                ft = ftiles[fti % 3]
                fti += 1
                P.cp("act", ft.ap[:, 0:256], ps.ap[:, 0:256], [ps], [ft])
                P.dma("pool", foutS[g][:, 0:256], ft.ap[:, 0:256], [ft], [foutS_b[g][0]])
        for st in range(4):
            ct, stt_ = ctabs[(st + 1) % 2]
            c3 = ct.ap.rearrange("p (k s) -> p k s", k=16)
            s3 = stt_.ap.rearrange("p (k s) -> p k s", k=16)
            cv = cosL_d.rearrange("(k p) s -> p k s", p=128)
            sv = sinL_d.rearrange("(k p) s -> p k s", p=128)
            for hh in range(2):
                P.dma("sp", c3[:, hh * 8:(hh + 1) * 8, :], cv[:, hh * 8:(hh + 1) * 8, st * 512:(st + 1) * 512], [], [ct])
                P.dma("sp", s3[:, hh * 8:(hh + 1) * 8, :], sv[:, hh * 8:(hh + 1) * 8, st * 512:(st + 1) * 512], [], [stt_])
            for g in range(4):
                ps = PS.next()
                for tc in range(16):
                    P.mm(ps.ap, Gcs[2 + tc].ap[:, g * 256:g * 256 + 128], c3[:, tc, :], tc == 0, False,
                         [Gcs[2 + tc], ct], [ps])
                    P.mm(ps.ap, Gcs[2 + tc].ap[:, g * 256 + 128:g * 256 + 256], s3[:, tc, :], False, tc == 15,
                         [Gcs[2 + tc], stt_], [ps])
                ft = ftiles[fti % 3]
                fti += 1
                P.cp("act", ft.ap, ps.ap, [ps], [ft])
                P.dma("pool", foutS[g][:, 256 + st * 512:256 + (st + 1) * 512], ft.ap, [ft], [foutS_b[g][1 + st]])
        if stop_after == "p3":
            break

        P.barrier()
        A.reset(arena_base)
        PSS = Rot(banks[2:8])

        def v3(tb):
            return tb.ap.rearrange("p (g i) -> p g i", i=64)

        def scan_dir(d):
            QN = ("RhT", "KhT", "kapT", "bhT", "btT", "ktT", "vT", "Dg")
            ops_ = {q: [A.bf16(256, q) for _ in range(4)] for q in QN}
            tw = A.f32(256, "tw")
            adu = A.f32(256, "adu")
            tn = ("r", "k", "v", "sig", "lw", "a", "kk", "sq", "nrm", "kap", "t1", "kdir", "beta", "cum", "L", "Lend", "Lprev",
                  "eL", "enL", "ePrev", "eEnd", "onesr")
            tp = {n: A.f32(256, n) for n in tn}
            gC = A.f32(4, "gC")
            P.memset("pool", tp["onesr"].ap, 1.0, [tp["onesr"]])
            pn = ("N", "NT", "N2", "NT2", "X", "AkkT", "BrkT", "Vtok", "kttok", "QT", "PhiT")
            pt = {n: A.bf16(512, n) for n in pn}
            for n in ("Y0T", "Psi", "Xf"):
                pt[n] = A.f32(512, n)
            W2 = A.bf16(1024, "W2")
            RHS1 = A.bf16(1024, "RHS1")
            negPU = A.bf16(1024, "negPU")
            Hs = [A.bf16(256, "H") for _ in range(2)]
            youts = [A.f32(256, "yout") for _ in range(2)]
            W2v = W2.ap.rearrange("p (g x) -> p g x", x=128)
            RHS1v = RHS1.ap.rearrange("p (g x) -> p g x", x=128)

            mT_s = masks.ap[:, (0 if d == 0 else 1) * 512:(0 if d == 0 else 1) * 512 + 512]
            mT_i = masks.ap[:, (2 if d == 0 else 3) * 512:(2 if d == 0 else 3) * 512 + 512]
            m_s = masks.ap[:, (1 if d == 0 else 0) * 512:(1 if d == 0 else 0) * 512 + 512]
            hi = 0
            P.memset("pool", Hs[0].ap, 0.0, [Hs[0]])
            units = list(range(9)) if d == 0 else [0] + list(range(8, 0, -1))
            for u in units:
                t0 = u * 256
                P.dma("sp", tw.ap, rwS[12][:, t0:t0 + 256], [rwS_b[12]], [tw])
                P.dma("sp", adu.ap, rwS[13][:, t0:t0 + 256], [rwS_b[13]], [adu])
                P.actf(tw.ap, tw.ap, AF.Tanh, [tw], [tw])
                for hp in range(4):
                    o = {q: ops_[q][hp] for q in QN}
                    P.dma("sp", tp["r"].ap, rwS[hp][:, t0:t0 + 256], [rwS_b[hp]], [tp["r"]])
                    P.dma("sp", tp["k"].ap, rwS[4 + hp][:, t0:t0 + 256], [rwS_b[4 + hp]], [tp["k"]])
                    P.dma("sp", tp["v"].ap, rwS[8 + hp][:, t0:t0 + 256], [rwS_b[8 + hp]], [tp["v"]])
                    P.cp("act", o["vT"].ap, tp["v"].ap, [tp["v"]], [o["vT"]])
                    pw = banks[d]
                    pb = PSS.next()
                    P.mm(pw.ap[:, 0:256], wup[l][d].ap[:, hp * 128:(hp + 1) * 128], tw.ap, True, True, [wup[l][d], tw], [pw])
                    P.mm(pw.ap[:, 256:512], aup[l][d].ap[:, hp * 128:(hp + 1) * 128], adu.ap, True, True, [aup[l][d], adu], [pw])
                    P.actf(tp["sig"].ap, pw.ap[:, 0:256], AF.Sigmoid, [pw, pv[l]], [tp["sig"]], bias=pv[l].ap[:, d * 4 + hp:d * 4 + hp + 1])
                    P.actf(tp["a"].ap, pw.ap[:, 256:512], AF.Sigmoid, [pw, pv[l]], [tp["a"]], bias=pv[l].ap[:, 8 + d * 4 + hp:8 + d * 4 + hp + 1])
                    P.ts("pool", tp["lw"].ap, tp["sig"].ap, NEG_E, None, ALU.mult, None, [tp["sig"]], [tp["lw"]])
                    P.ts("pool", tp["kk"].ap, tp["k"].ap, pv[l].ap[:, 16 + hp:17 + hp], None, ALU.mult, None, [tp["k"], pv[l]], [tp["kk"]])
                    P.tt("pool", tp["sq"].ap, tp["kk"].ap, tp["kk"].ap, ALU.mult, [tp["kk"]], [tp["sq"]])
                    P.mm(pb.ap[:, 0:256], onesblk.ap, tp["sq"].ap, True, True, [onesblk, tp["sq"]], [pb])
                    P.actf(tp["nrm"].ap, pb.ap[:, 0:256], AF.Sqrt, [pb], [tp["nrm"]])
                    P.ts("dve", tp["nrm"].ap, tp["nrm"].ap, 1e-12, None, ALU.max, None, [tp["nrm"]], [tp["nrm"]])
                    P.recip(tp["nrm"].ap, tp["nrm"].ap, [tp["nrm"]], [tp["nrm"]])
                    yield
                    P.tt("dve", tp["kap"].ap, tp["kk"].ap, tp["nrm"].ap, ALU.mult, [tp["kk"], tp["nrm"]], [tp["kap"]])
                    P.ts("pool", tp["t1"].ap, tp["a"].ap, pv[l].ap[:, 20 + hp:21 + hp], omka[l].ap[:, hp:hp + 1], ALU.mult, ALU.add,
                         [tp["a"], pv[l], omka[l]], [tp["t1"]])
                    P.tt("pool", tp["kdir"].ap, tp["k"].ap, tp["t1"].ap, ALU.mult, [tp["k"], tp["t1"]], [tp["kdir"]])
                    P.tt("dve", tp["beta"].ap, tp["kap"].ap, tp["a"].ap, ALU.mult, [tp["kap"], tp["a"]], [tp["beta"]])
                    P.scan(tp["cum"].ap, tp["onesr"].ap, tp["lw"].ap, [tp["onesr"], tp["lw"]], [tp["cum"]])
                    yield
                    cum3, L3, lw3 = v3(tp["cum"]), v3(tp["L"]), v3(tp["lw"])
                    if d == 0:
                        P.cp("pool", tp["L"].ap[:, 0:64], tp["cum"].ap[:, 0:64], [tp["cum"]], [tp["L"]])
                        P.tt("dve", L3[:, 1:4, :], cum3[:, 1:4, :], cum3[:, 0:3, 63:64].broadcast_to([128, 3, 64]), ALU.subtract,
                             [tp["cum"]], [tp["L"]])
                        Ltot = L3[:, :, 63:64]
                    else:
                        P.tt("pool", tp["L"].ap, tp["lw"].ap, tp["cum"].ap, ALU.subtract, [tp["lw"], tp["cum"]], [tp["L"]])
                        P.tt("dve", L3, L3, cum3[:, :, 63:64].broadcast_to([128, 4, 64]), ALU.add, [tp["L"], tp["cum"]], [tp["L"]])
                        Ltot = L3[:, :, 0:1]
                    P.tt("dve", v3(tp["Lend"]), Ltot.broadcast_to([128, 4, 64]), L3, ALU.subtract, [tp["L"]], [tp["Lend"]])
                    P.tt("pool", tp["Lprev"].ap, tp["L"].ap, tp["lw"].ap, ALU.subtract, [tp["L"], tp["lw"]], [tp["Lprev"]])
                    P.actf(tp["eL"].ap, tp["L"].ap, AF.Exp, [tp["L"]], [tp["eL"]])
                    P.actf(tp["enL"].ap, tp["L"].ap, AF.Exp, [tp["L"]], [tp["enL"]], scale=-1.0)
                    P.actf(tp["ePrev"].ap, tp["Lprev"].ap, AF.Exp, [tp["Lprev"]], [tp["ePrev"]])
                    P.actf(tp["eEnd"].ap, tp["Lend"].ap, AF.Exp, [tp["Lend"]], [tp["eEnd"]])
                    P.actf(gC.ap.rearrange("p (c o) -> p c o", o=1), Ltot, AF.Exp, [tp["L"]], [gC])
                    yield
                    P.tt("dve", o["RhT"].ap, tp["r"].ap, tp["eL"].ap, ALU.mult, [tp["r"], tp["eL"]], [o["RhT"]])
                    P.tt("pool", o["KhT"].ap, tp["kdir"].ap, tp["enL"].ap, ALU.mult, [tp["kdir"], tp["enL"]], [o["KhT"]])
                    P.tt("dve", o["kapT"].ap, tp["kap"].ap, tp["ePrev"].ap, ALU.mult, [tp["kap"], tp["ePrev"]], [o["kapT"]])
                    P.tt("pool", o["bhT"].ap, tp["beta"].ap, tp["enL"].ap, ALU.mult, [tp["beta"], tp["enL"]], [o["bhT"]])
                    P.tt("dve", o["btT"].ap, tp["beta"].ap, tp["eEnd"].ap, ALU.mult, [tp["beta"], tp["eEnd"]], [o["btT"]])
                    P.tt("pool", o["ktT"].ap, tp["kdir"].ap, tp["eEnd"].ap, ALU.mult, [tp["kdir"], tp["eEnd"]], [o["ktT"]])
                    yield
                    for c in range(4):
                        P.ts("pool", o["Dg"].ap[:, c * 64:(c + 1) * 64], ident8.ap[:, 0:64], gC.ap[:, c:c + 1], None, ALU.mult, None,
                             [ident8, gC], [o["Dg"]])
                pairs = [(0, 1), (2, 3)] if d == 0 else [(3, 2), (1, 0)]
                for cs in pairs:
                    def each():
                        for ci, c in enumerate(cs):
                            for hp in range(4):
                                for par in range(2):
                                    yield ci, c, hp, par * 64, ci * 4 + hp

                    def sl(tb, po, g, w=64, off=0):
                        return tb.ap[po:po + 64, g * w + off:g * w + off + 64]

                    def osl(q, hp, po, c):
                        return ops_[q][hp].ap[po:po + 64, c * 64:(c + 1) * 64]

                    def amat(lq, rq):
                        ps = PSS.next()
                        for ci, c, hp, po, g in each():
                            P.mm(ps.ap[po:po + 64, g * 64:(g + 1) * 64], osl(lq, hp, po, c), osl(rq, hp, po, c), True, True,
                                 [ops_[lq][hp], ops_[rq][hp]], [ps])
                        return ps

                    ps = amat("bhT", "kapT")
                    P.stt(pt["N"].ap, ps.ap, -1.0, mT_s, ALU.mult, ALU.mult, [ps, masks], [pt["N"]])
                    yield
                    ps = amat("kapT", "bhT")
                    P.stt(pt["NT"].ap, ps.ap, -1.0, m_s, ALU.mult, ALU.mult, [ps, masks], [pt["NT"]])
                    yield
                    ps = amat("KhT", "kapT")
                    P.tt("dve", pt["AkkT"].ap, ps.ap, mT_s, ALU.mult, [ps, masks], [pt["AkkT"]])
                    yield
                    ps = amat("KhT", "RhT")
                    P.tt("dve", pt["BrkT"].ap, ps.ap, mT_i, ALU.mult, [ps, masks], [pt["BrkT"]])
                    yield
                    ps = amat("bhT", "RhT")
                    P.tt("dve", W2v[:, :, 0:64], v3(ps), mT_i.rearrange("p (g i) -> p g i", i=64), ALU.mult, [ps, masks], [W2])
                    yield
                    P.tt("pool", pt["Xf"].ap, pt["N"].ap, ident8.ap, ALU.add, [pt["N"], ident8], [pt["Xf"]])
                    P.tt("dve", pt["X"].ap, pt["N"].ap, ident8.ap, ALU.add, [pt["N"], ident8], [pt["X"]])
                    cur = (pt["N"], pt["NT"])
                    nxt = (pt["N2"], pt["NT2"])
                    for m in range(1, 6):
                        if m < 5:
                            ps = PSS.next()
                            for ci, c, hp, po, g in each():
                                P.mm(ps.ap[po:po + 64, g * 64:(g + 1) * 64], sl(cur[1], po, g), sl(cur[0], po, g), True, True,
                                     [cur[0], cur[1]], [ps])
                            P.cp("act", nxt[0].ap, ps.ap, [ps], [nxt[0]])
                            yield
                        ps = PSS.next()
                        for ci, c, hp, po, g in each():
                            P.mm(ps.ap[po:po + 64, g * 64:(g + 1) * 64], sl(cur[0], po, g), sl(cur[1], po, g), True, True,
                                 [cur[0], cur[1]], [ps])
                        P.cp("act", nxt[1].ap, ps.ap, [ps], [nxt[1]])
                        yield
                        ps = PSS.next()
                        for ci, c, hp, po, g in each():
                            P.mm(ps.ap[po:po + 64, g * 64:(g + 1) * 64], sl(nxt[1], po, g), sl(pt["X"], po, g), True, True,
                                 [nxt[1], pt["X"]], [ps])
                        P.tt("dve", pt["X"].ap, ps.ap, pt["Xf"].ap, ALU.add, [ps, pt["Xf"]], [pt["X"]])
                        yield
                        if m < 5:
                            P.tt("dve", pt["Xf"].ap, ps.ap, pt["Xf"].ap, ALU.add, [ps, pt["Xf"]], [pt["Xf"]])
                        cur, nxt = nxt, cur
                    for qi, (srcq, dst3, dtb) in enumerate((("kapT", RHS1v[:, :, 0:64], RHS1), ("vT", v3(pt["Vtok"]), pt["Vtok"]),
                                                            ("btT", W2v[:, :, 64:128], W2), ("ktT", v3(pt["kttok"]), pt["kttok"]))):
                        ps = PSS.next()
                        for ci, c, hp, po, g in each():
                            P.mm(ps.ap[po:po + 64, g * 64:(g + 1) * 64], osl(srcq, hp, po, c), identb.ap[po:po + 64, po:po + 64], True, True,
                                 [ops_[srcq][hp], identb], [ps])
                        P.cp("act" if qi % 2 == 0 else "dve", dst3, v3(ps), [ps], [dtb])
                        yield
                    ps = PSS.next()
                    for ci, c, hp, po, g in each():
                        P.mm(ps.ap[po:po + 64, g * 64:(g + 1) * 64], sl(pt["AkkT"], po, g), sl(pt["Vtok"], po, g), True, True,
                             [pt["AkkT"], pt["Vtok"]], [ps])
                    P.cp("act", RHS1v[:, :, 64:128], v3(ps), [ps], [RHS1])
                    yield
                    for ci in range(2):
                        ps = PSS.next()
                        for hp in range(4):
                            for par in range(2):
                                po = par * 64
                                g = ci * 4 + hp
                                P.mm(ps.ap[po:po + 64, hp * 128:(hp + 1) * 128], sl(pt["X"], po, g),
                                     RHS1.ap[po:po + 64, g * 128:(g + 1) * 128], True, True, [pt["X"], RHS1], [ps])
                        P.ts("dve", negPU.ap[:, ci * 512:(ci + 1) * 512], ps.ap, -1.0, None, ALU.mult, None, [ps], [negPU])
                        yield
                    ps = PSS.next()
                    for ci, c, hp, po, g in each():
                        P.mm(ps.ap[po:po + 64, g * 64:(g + 1) * 64], identb.ap[po:po + 64, po:po + 64], osl("RhT", hp, po, c), True, False,
                             [identb, ops_["RhT"][hp]], [ps])
                        P.mm(ps.ap[po:po + 64, g * 64:(g + 1) * 64], sl(negPU, po, g, 128, 0), sl(W2, po, g, 128, 0), False, True,
                             [negPU, W2], [ps])
                    P.cp("act", pt["QT"].ap, ps.ap, [ps], [pt["QT"]])
                    yield
                    ps = PSS.next()
                    for ci, c, hp, po, g in each():
                        P.mm(ps.ap[po:po + 64, g * 64:(g + 1) * 64], identb.ap[po:po + 64, po:po + 64], osl("Dg", hp, po, c), True, False,
                             [identb, ops_["Dg"][hp]], [ps])
                        P.mm(ps.ap[po:po + 64, g * 64:(g + 1) * 64], sl(negPU, po, g, 128, 0), sl(W2, po, g, 128, 64), False, True,
                             [negPU, W2], [ps])
                    P.cp("dve", pt["PhiT"].ap, ps.ap, [ps], [pt["PhiT"]])
                    yield
                    ps = PSS.next()
                    for ci, c, hp, po, g in each():
                        P.mm(ps.ap[po:po + 64, g * 64:(g + 1) * 64], sl(pt["Vtok"], po, g), sl(pt["BrkT"], po, g), True, False,
                             [pt["Vtok"], pt["BrkT"]], [ps])
                        P.mm(ps.ap[po:po + 64, g * 64:(g + 1) * 64], sl(negPU, po, g, 128, 64), sl(W2, po, g, 128, 0), False, True,
                             [negPU, W2], [ps])
                    P.cp("act", pt["Y0T"].ap, ps.ap, [ps], [pt["Y0T"]])
                    yield
                    ps = PSS.next()
                    for ci, c, hp, po, g in each():
                        P.mm(ps.ap[po:po + 64, g * 64:(g + 1) * 64], sl(pt["kttok"], po, g), sl(pt["Vtok"], po, g), True, False,
                             [pt["kttok"], pt["Vtok"]], [ps])
                        P.mm(ps.ap[po:po + 64, g * 64:(g + 1) * 64], sl(W2, po, g, 128, 64), sl(negPU, po, g, 128, 64), False, True,
                             [W2, negPU], [ps])
                    P.cp("dve", pt["Psi"].ap, ps.ap, [ps], [pt["Psi"]])
                    yield
                    for ci, c in enumerate(cs):
                        Hc, Hn = Hs[hi % 2], Hs[(hi + 1) % 2]
                        yo = youts[hi % 2]
                        hi += 1
                        ps = PSS.next()
                        for hp in range(4):
                            for par in range(2):
                                po = par * 64
                                g = ci * 4 + hp
                                P.mm(ps.ap[po:po + 64, hp * 64:(hp + 1) * 64], sl(Hc, po, hp), sl(pt["QT"], po, g), True, True,
                                     [Hc, pt["QT"]], [ps])
                                P.mm(ps.ap[po:po + 64, 256 + hp * 64:256 + (hp + 1) * 64], sl(pt["PhiT"], po, g), sl(Hc, po, hp), True, True,
                                     [Hc, pt["PhiT"]], [ps])
                        P.tt("dve", yo.ap, ps.ap[:, 0:256], pt["Y0T"].ap[:, ci * 256:(ci + 1) * 256], ALU.add, [ps, pt["Y0T"]], [yo])
                        P.tt("dve", Hn.ap, ps.ap[:, 256:512], pt["Psi"].ap[:, ci * 256:(ci + 1) * 256], ALU.add, [ps, pt["Psi"]], [Hn])
                        tg = t0 + c * 64
                        P.dma("sp", yS[d][:, :, tg:tg + 64], v3(yo), [yo], [yS_b[d][u * 4 + c]])
                        yield
        def precast_gen(l=l):
            wl = WL(4)
            jobs = []
            for fc in range(8):
                jobs.append((wview(wf_d[l], fc * 128, 0, 4), 4, wfB[fc], wfB_b[fc]))
                jobs.append((wview(wr_d[l], fc * 128, 0, 4), 4, wrB[fc], wrB_b[fc]))
                jobs.append((wview(wo_d[l], fc * 128), 8, woB[fc], woB_b[fc]))
            for fc in range(32):
                jobs.append((wview(w1_d[l], fc * 128), 8, w1B[fc], w1B_b[fc]))
            for fc in range(8):
                for q in range(4):
                    jobs.append((wview(w2_d[l], fc * 128, q * 8, 8), 8, w2B[fc * 4 + q], w2B_b[fc * 4 + q]))
            for (src3, kc, dst, dstb), wb in zip(jobs, prefetched(wl, [(j[0], j[1]) for j in jobs])):
                P.dma("sp", dst[:, 0:kc * 128], wb.ap[:, 0:kc * 128], [wb], [dstb])
                yield
                yield
                yield

        gens = [scan_dir(0), scan_dir(1), precast_gen()]
        while gens:
            for g_ in list(gens):
                try:
                    next(g_)
                except StopIteration:
                    gens.remove(g_)
        if stop_after == "p2":
            break

        for ti, (t0, nt) in enumerate(TILES):
            if last and ti == 0:
                continue
            col = 1 if ti == 0 else 0
            P.barrier()
            A.reset(arena_base)
            xt = A.f32(4096, "xt")
            x3 = xt.ap[:, 0:8 * nt].rearrange("p (k t) -> p k t", k=8)
            P.dma("pool", x3, xsrc[:, :, t0:t0 + nt], [xS_b[ti]] if l > 0 else [], [xt])
            rwk = A.bf16(4 * 512, "rwk")
            rwk3 = rwk.ap[:, 0:4 * nt].rearrange("p (g t) -> p g t", g=4)
            mark2 = A.off
            y0 = A.f32(2048, "y0")
            y1 = A.f32(2048, "y1")
            y03 = y0.ap[:, 0:4 * nt].rearrange("p (g t) -> p g t", g=4)
            y13 = y1.ap[:, 0:4 * nt].rearrange("p (g t) -> p g t", g=4)
            cl = [u_ * 4 + c_ for u_ in range(t0 // 256, (t0 + nt) // 256) for c_ in range(4)]
            P.dma("sp", y03, yS[0][:, :, t0:t0 + nt], [yS_b[0][c_] for c_ in cl], [y0])
            P.dma("sp", y13, yS[1][:, :, t0:t0 + nt], [yS_b[1][c_] for c_ in cl], [y1])
            P.tt("pool", y0.ap[:, 0:4 * nt], y0.ap[:, 0:4 * nt], y1.ap[:, 0:4 * nt], ALU.add, [y0, y1], [y0])
            adt = A.f32(512, "adt")
            sgt = A.f32(512, "sgt")
            P.dma("sp", adt.ap[:, 0:nt], rwS[13][:, t0:t0 + nt], [rwS_b[13]], [adt])
            P.dma("sp", sgt.ap[:, 0:nt], rwS[14][:, t0:t0 + nt], [rwS_b[14]], [sgt])
            P.actf(sgt.ap[:, 0:nt], sgt.ap[:, 0:nt], AF.Sigmoid, [sgt], [sgt])
            cn = ("r", "k", "v", "yc", "sq", "sd", "yn", "a0", "a1", "tq", "ks", "prod", "bon")
            ct_ = [{n: A.f32(512, n) for n in cn} for _ in range(2)]
            def p2c_hp(hp):
                tq = ct_[hp % 2]
                w = {n: tq[n].ap[:, 0:nt] for n in cn}
                P.dma("sp", w["r"], rwS[hp][:, t0:t0 + nt], [rwS_b[hp]], [tq["r"]])
                P.dma("sp", w["k"], rwS[4 + hp][:, t0:t0 + nt], [rwS_b[4 + hp]], [tq["k"]])
                P.dma("sp", w["v"], rwS[8 + hp][:, t0:t0 + nt], [rwS_b[8 + hp]], [tq["v"]])
                pA = PS.next()
                P.mm(pA.ap[:, 0:nt], onesblkm.ap, y03[:, hp, :], True, True, [onesblkm, y0], [pA])
                P.tt("dve", w["yc"], y03[:, hp, :], pA.ap[:, 0:nt], ALU.subtract, [y0, pA], [tq["yc"]])
                yield
                P.actf(w["sq"], w["yc"], AF.Square, [tq["yc"]], [tq["sq"]])
                pB = PS.next()
                P.mm(pB.ap[:, 0:nt], onesblkm.ap, w["sq"], True, True, [onesblkm, tq["sq"]], [pB])
                P.actf(w["sd"], pB.ap[:, 0:nt], AF.Sqrt, [pB, eps_g], [tq["sd"]], bias=eps_g.ap)
                yield
                P.recip(w["sd"], w["sd"], [tq["sd"]], [tq["sd"]])
                P.tt("dve", w["yn"], w["yc"], w["sd"], ALU.mult, [tq["yc"], tq["sd"]], [tq["yn"]])
                P.actf(w["yn"], w["yn"], AF.Identity, [tq["yn"], pv[l]], [tq["yn"]], bias=pv[l].ap[:, 32 + hp:33 + hp],
                       scale=pv[l].ap[:, 28 + hp:29 + hp])
                pC = PS.next()
                P.mm(pC.ap[:, 0:nt], aup[l][0].ap[:, hp * 128:(hp + 1) * 128], adt.ap[:, 0:nt], True, True, [aup[l][0], adt], [pC])
                P.actf(w["a0"], pC.ap[:, 0:nt], AF.Sigmoid, [pC, pv[l]], [tq["a0"]], bias=pv[l].ap[:, 8 + hp:9 + hp])
                yield
                pD = PS.next()
                P.mm(pD.ap[:, 0:nt], aup[l][1].ap[:, hp * 128:(hp + 1) * 128], adt.ap[:, 0:nt], True, True, [aup[l][1], adt], [pD])
                P.actf(w["a1"], pD.ap[:, 0:nt], AF.Sigmoid, [pD, pv[l]], [tq["a1"]], bias=pv[l].ap[:, 12 + hp:13 + hp])
                yield
                P.tt("pool", w["a0"], w["a0"], w["a1"], ALU.add, [tq["a0"], tq["a1"]], [tq["a0"]])
                P.ts("dve", w["tq"], w["a0"], pv[l].ap[:, 20 + hp:21 + hp], c2[l].ap[:, hp:hp + 1], ALU.mult, ALU.add,
                     [tq["a0"], pv[l], c2[l]], [tq["tq"]])
                P.tt("pool", w["ks"], w["k"], w["tq"], ALU.mult, [tq["k"], tq["tq"]], [tq["ks"]])
                P.stt(w["prod"], w["r"], pv[l].ap[:, 24 + hp:25 + hp], w["ks"], ALU.mult, ALU.mult, [tq["r"], pv[l], tq["ks"]], [tq["prod"]])
                pE = PS.next()
                P.mm(pE.ap[:, 0:nt], onesblk.ap, w["prod"], True, True, [onesblk, tq["prod"]], [pE])
                P.tt("dve", w["bon"], w["v"], pE.ap[:, 0:nt], ALU.mult, [tq["v"], pE], [tq["bon"]])
                yield
                P.tt("pool", w["bon"], w["bon"], w["yn"], ALU.add, [tq["bon"], tq["yn"]], [tq["bon"]])
                pG = PS.next()
                P.mm(pG.ap[:, 0:nt], gup[l].ap[:, hp * 128:(hp + 1) * 128], sgt.ap[:, 0:nt], True, True, [gup[l], sgt], [pG])
                P.tt("dve", rwk3[:, hp, :], w["bon"], pG.ap[:, 0:nt], ALU.mult, [tq["bon"], pG], [rwk])
                yield
            for hpp in ((0, 1), (2, 3)):
                gens = [p2c_hp(hpp[0]), p2c_hp(hpp[1])]
                while gens:
                    for g_ in list(gens):
                        try:
                            next(g_)
                        except StopIteration:
                            gens.remove(g_)
            if dbg and l == 0:
                P.dma("sp", dbg_rwk[:, :, t0:t0 + nt], rwk3, [rwk], [])
            P.barrier()
            A.reset(mark2)
            wl = BL(5)
            fo = A.bf16(4 * 512, "fo")
            fo3 = fo.ap[:, 0:4 * nt].rearrange("p (g t) -> p g t", g=4)
            P.dma("sp", fo3, foutS.rearrange("g p t -> p g t")[:, :, t0:t0 + nt], [foutS_b[g][ti] for g in range(4)], [fo])
            gt = A.bf16(16 * 512, "gt")
            gt3 = gt.ap[:, 0:16 * nt].rearrange("p (g t) -> p g t", g=16)
            P.dma("pool", gt3, gateS.rearrange("g p t -> p g t")[:, :, t0:t0 + nt], gateS_b, [gt])
            mg = A.bf16(8 * 512, "mg")
            mg3 = mg.ap[:, 0:8 * nt].rearrange("p (g t) -> p g t", g=8)
            m1s = [A.f32(512, "m1") for _ in range(2)]
            m2s = [A.f32(512, "m2") for _ in range(2)]
            specs = []
            for fc in range(8):
                specs.append((wfB[fc], 512, wfB_b[fc]))
                specs.append((wrB[fc], 512, wrB_b[fc]))
            it = prefetched(wl, specs)
            for fc in range(8):
                wfb = next(it)
                wrb = next(it)
                wf3 = wfb.ap[:, 0:512].rearrange("p (k n) -> p k n", k=4)
                wr3 = wrb.ap[:, 0:512].rearrange("p (k n) -> p k n", k=4)
                pF, pR = PS.next(), PS.next()
                for kc in range(4):
                    P.mm(pF.ap[:, 0:nt], wf3[:, kc, :], fo3[:, kc, :], kc == 0, kc == 3, [wfb, fo], [pF])
                for kc in range(4):
                    P.mm(pR.ap[:, 0:nt], wr3[:, kc, :], rwk3[:, kc, :], kc == 0, kc == 3, [wrb, rwk], [pR])
                m1, m2 = m1s[fc % 2], m2s[fc % 2]
                P.tt("dve", m1.ap[:, 0:nt], pF.ap[:, 0:nt], gt3[:, fc, :], ALU.mult, [pF, gt], [m1])
                P.tt("dve", m2.ap[:, 0:nt], pR.ap[:, 0:nt], gt3[:, 8 + fc, :], ALU.mult, [pR, gt], [m2])
                P.tt("pool", mg3[:, fc, :], m1.ap[:, 0:nt], m2.ap[:, 0:nt], ALU.add, [m1, m2], [mg])
            specs = [(woB[fc], 1024, woB_b[fc]) for fc in range(8)]
            for fc, wob in enumerate(prefetched(wl, specs)):
                wo3 = wob.ap.rearrange("p (k n) -> p k n", k=8)
                ps = PS.next()
                for kc in range(8):
                    P.mm(ps.ap[:, 0:nt], wo3[:, kc, :], mg3[:, kc, :], kc == 0, kc == 7, [wob, mg], [ps])
                P.stt(x3[:, fc, :], ps.ap[:, 0:nt], modcol(l, 2, fc, col), x3[:, fc, :], ALU.mult, ALU.add, [ps, mod[l], xt], [xt])
            if dbg and l == 0:
                P.dma("sp", dbg_mg[:, :, t0:t0 + nt], mg3, [mg], [])
                P.dma("sp", dbg_xmid[:, :, t0:t0 + nt], x3, [xt], [])
            P.barrier()
            A.reset(mark2)
            wl = BL(5)
            sq = A.f32(4096, "sq")
            rstd = A.f32(512, "rstd")
            tmp = [A.f32(512, "tmp") for _ in range(2)]
            h2 = A.bf16(8 * 512, "h2")
            h23 = h2.ap[:, 0:8 * nt].rearrange("p (k t) -> p k t", k=8)
            norm_mod(l, 1, 3, col, x3, nt, sq, rstd, tmp, xt, h23, h2)
            uu = A.bf16(32 * 512, "u")
            u3 = uu.ap[:, 0:32 * nt].rearrange("p (k t) -> p k t", k=32)
            rl = [A.f32(512, "relu") for _ in range(2)]
            specs = [(w1B[fc], 1024, w1B_b[fc]) for fc in range(32)]
            for fc, w1b in enumerate(prefetched(wl, specs)):
                w13 = w1b.ap.rearrange("p (k n) -> p k n", k=8)
                ps = PS.next()
                for kc in range(8):
                    P.mm(ps.ap[:, 0:nt], w13[:, kc, :], h23[:, kc, :], kc == 0, kc == 7, [w1b, h2], [ps])
                r_ = rl[fc % 2]
                P.actf(r_.ap[:, 0:nt], ps.ap[:, 0:nt], AF.Relu, [ps], [r_])
                P.tt("pool", u3[:, fc, :], r_.ap[:, 0:nt], r_.ap[:, 0:nt], ALU.mult, [r_], [uu])
            specs = [(w2B[fc * 4 + q], 1024, w2B_b[fc * 4 + q]) for fc in range(8) for q in range(4)]
            it = prefetched(wl, specs)
            for fc in range(8):
                ps = PS.next()
                for q in range(4):
                    w2b = next(it)
                    w23 = w2b.ap.rearrange("p (k n) -> p k n", k=8)
                    for kk in range(8):
                        P.mm(ps.ap[:, 0:nt], w23[:, kk, :], u3[:, q * 8 + kk, :], q == 0 and kk == 0, q == 3 and kk == 7, [w2b, uu], [ps])
                P.stt(x3[:, fc, :], ps.ap[:, 0:nt], modcol(l, 5, fc, col), x3[:, fc, :], ALU.mult, ALU.add, [ps, mod[l], xt], [xt])
            if not last:
                P.dma("sp", xS[:, :, t0:t0 + nt], x3, [xt], [xS_b[ti]])
            else:
                rms_rstd(x3, nt, sq, rstd, xt)
                sq3 = sq.ap[:, 0:8 * nt].rearrange("p (k t) -> p k t", k=8)
                for k in range(8):
                    P.stt(sq3[:, k, :], x3[:, k, :], norms.ap[:, 32 + k:33 + k], rstd.ap[:, 0:nt], ALU.mult, ALU.mult,
                          [xt, norms, rstd], [sq])
                P.dma("sp", out_d[:, :, t0 - 256:t0 - 256 + nt], sq3, [sq], [])
    P.emit()
    return nc


def _chunk128(v):
    return np.ascontiguousarray(v.reshape(-1, 128).T)


def _consts():
    idx = np.arange(64)
    mU = (idx[:, None] < idx[None, :]).astype(np.float32)
    mL = (idx[:, None] > idx[None, :]).astype(np.float32)
    eye = np.eye(64, dtype=np.float32)

    def tile8(m):
        return np.tile(np.tile(m, (2, 1))[:, None, :], (1, 8, 1)).reshape(128, 512)

    masks = np.concatenate([tile8(mU), tile8(mL), tile8(mU + eye), tile8(mL + eye)], axis=1)
    ident8 = tile8(eye)
    onesblk = np.zeros((128, 128), np.float32)
    onesblk[:64, :64] = 1.0
    onesblk[64:, 64:] = 1.0
    n = np.arange(128)
    ang = 2.0 * np.pi * np.outer(n, n) / 128.0
    dftc = np.concatenate([np.cos(ang), np.sin(ang)], axis=1) / np.sqrt(128.0)

    def seq_tabs(L):
        t = np.arange(L)
        a = 2.0 * np.pi * ((np.outer(t, t)) % L) / L
        return (np.cos(a) / np.sqrt(L)).astype(ml_dtypes.bfloat16), (-np.sin(a) / np.sqrt(L)).astype(ml_dtypes.bfloat16)

    cosL, sinL = seq_tabs(2048)
    cosC, sinC = seq_tabs(256)
    return {
        "ident": np.eye(128, dtype=np.float32), "onesblk": onesblk, "masks": np.ascontiguousarray(masks),
        "ident8": ident8, "dftc": dftc.astype(ml_dtypes.bfloat16), "cosL": cosL, "sinL": sinL, "cosC": cosC, "sinC": sinC,
    }


def make_in_maps(x, c, ctx, c_ctx, w_mod, b_mod, norm1, norm2, w_in, mu_shift, w0, w_up, a0, a_up, g_up, k_k, k_a, r_k,
                 ln_x_w, ln_x_b, w_fourier_up, w_rwkv_up, w_out, mlp_w1, mlp_w2, norm_f):
    f = lambda a: np.ascontiguousarray(np.asarray(a, dtype=np.float32))
    x, c, ctx, c_ctx = f(x), f(c), f(ctx), f(c_ctx)
    shared = dict(_consts())
    shared["w_mod"] = f(w_mod)
    shared["bmod"] = np.stack([_chunk128(f(b_mod)[l]) for l in range(NL)])
    shared["norms"] = np.concatenate([_chunk128(f(norm1)[0]), _chunk128(f(norm1)[1]), _chunk128(f(norm2)[0]),
                                      _chunk128(f(norm2)[1]), _chunk128(f(norm_f))], axis=1)
    shared["w_in"] = f(w_in)
    shared["mu"] = np.stack([_chunk128(f(mu_shift)[l][:1920]) for l in range(NL)])
    pv = []
    for l in range(NL):
        cols = [_chunk128(f(w0)[l, 0]), _chunk128(f(w0)[l, 1]), _chunk128(f(a0)[l, 0]), _chunk128(f(a0)[l, 1]),
                _chunk128(f(k_k)[l]), _chunk128(f(k_a)[l]), _chunk128(f(r_k)[l].reshape(-1)), _chunk128(f(ln_x_w)[l]),
                _chunk128(f(ln_x_b)[l])]
        pv.append(np.concatenate(cols, axis=1))
    shared["pv"] = np.stack(pv)
    wup = np.zeros((NL, 2, 128, 512), np.float32)
    aup = np.zeros((NL, 2, 128, 512), np.float32)
    for l in range(NL):
        for d in range(2):
            wup[l, d, d * 64:(d + 1) * 64] = f(w_up)[l, d]
            aup[l, d, d * 64:(d + 1) * 64] = f(a_up)[l, d]
    shared["wup"] = wup
    shared["aup"] = aup
    shared["gup"] = f(g_up)
    shared["w_f"] = f(w_fourier_up)
    shared["w_r"] = f(w_rwkv_up)
    shared["w_o"] = f(w_out)
    shared["w1"] = f(mlp_w1)
    shared["w2"] = f(mlp_w2)
    in_maps = []
    for b in range(x.shape[0]):
        X = np.concatenate([ctx[b], x[b]], axis=0)
        xT = np.ascontiguousarray(X.T.reshape(8, 128, T).transpose(1, 0, 2))
        cv = np.stack([_chunk128(c[b]), _chunk128(c_ctx)], axis=2).reshape(128, 16)
        m = dict(shared)
        m["xT"] = xT
        m["cvec"] = np.ascontiguousarray(cv)
        in_maps.append(m)
    return in_maps


_NC_CACHE = {}


def kernel(**inputs):
    in_maps = make_in_maps(**inputs)
    if "nc" not in _NC_CACHE:
        _NC_CACHE["nc"] = build_program()
    nc = _NC_CACHE["nc"]
    res = run_bass_kernel_spmd(nc, in_maps, core_ids=list(range(len(in_maps))))
    outs = []
    for r in res.results:
        oT = np.asarray(r["outT"])
        outs.append(oT.transpose(1, 0, 2).reshape(1024, TLAT).T)
    return np.ascontiguousarray(np.stack(outs).astype(np.float32))
```
